# Optimizing a Trainium2 kernel written in Bass

```python
import jax, jax.numpy as jnp
from jax import lax
import numpy as np

D_MODEL = 1024
BATCH = 8
SEQ = 4096
DEPTH = 4

N_MIXERS = 2
CONV_KERNEL = 31
ATTN_GROUPS = ((128, 1), (512, 4), (2048, 16))
N_GROUPS = len(ATTN_GROUPS)
HEADS_PER_GROUP = 8
HEAD_DIM = 64
ATTN_WIDTH = HEADS_PER_GROUP * HEAD_DIM
QKV_WIDTH = 3 * N_GROUPS * ATTN_WIDTH
D_FF = 4 * D_MODEL
ROPE_THETA = 10000.0
EPS = 1e-6
N_CONV_LAYERS = (DEPTH + 1) // 2
N_ATTN_LAYERS = DEPTH // 2

kernel_name = "interleaved_conformer_conv_dilated_attention_trunk"


def rmsnorm(x, g):
    xf = x.astype(jnp.float32)
    y = xf * lax.rsqrt(jnp.mean(xf * xf, axis=-1, keepdims=True) + EPS)
    return (y * g.astype(jnp.float32)).astype(x.dtype)


def layernorm(x, g, b):
    xf = x.astype(jnp.float32)
    mu = jnp.mean(xf, axis=-1, keepdims=True)
    xc = xf - mu
    y = xc * lax.rsqrt(jnp.mean(xc * xc, axis=-1, keepdims=True) + EPS)
    return (y * g.astype(jnp.float32) + b.astype(jnp.float32)).astype(x.dtype)


def rope_tables(seq):
    pos = jnp.arange(seq, dtype=jnp.float32)
    inv_freq = ROPE_THETA ** (-jnp.arange(0, HEAD_DIM, 2, dtype=jnp.float32) / HEAD_DIM)
    ang = pos[:, None] * inv_freq[None, :]
    return jnp.cos(ang), jnp.sin(ang)


def apply_rope(t, cos, sin):
    tf = t.astype(jnp.float32)
    t1, t2 = jnp.split(tf, 2, axis=-1)
    c = cos[:, None, None, :]
    s = sin[:, None, None, :]
    return jnp.concatenate([t1 * c - t2 * s, t2 * c + t1 * s], axis=-1).astype(t.dtype)


def dilated_window_attention(q, k, v, window, dilation):
    b, s, h, e = q.shape
    w = window // dilation
    n_sub = s // dilation
    n_blk = -(-n_sub // w)
    pad = n_blk * w - n_sub

    def to_sub(t):
        t = t.reshape(b, n_sub, dilation, h, e).transpose(0, 2, 3, 1, 4)
        t = jnp.pad(t, ((0, 0), (0, 0), (0, 0), (0, pad), (0, 0)))
        return t.reshape(b, dilation, h, n_blk, w, e)

    def with_prev(t):
        prev = jnp.pad(t, ((0, 0), (0, 0), (0, 0), (1, 0), (0, 0), (0, 0)))[:, :, :, :-1]
        return jnp.concatenate([prev, t], axis=4)

    qs = to_sub(q)
    kb = with_prev(to_sub(k))
    vb = with_prev(to_sub(v))
    scores = jnp.einsum('brhnqe,brhnke->brhnqk', qs, kb).astype(jnp.float32) * (e ** -0.5)
    qi = jnp.arange(w)[None, :, None]
    kj = jnp.arange(2 * w)[None, None, :]
    blk = jnp.arange(n_blk)[:, None, None]
    valid = (kj >= qi) & (kj <= qi + w) & ((blk > 0) | (kj >= w))
    scores = jnp.where(valid, scores, -jnp.inf)
    m = jnp.max(scores, axis=-1, keepdims=True)
    p = jnp.exp(scores - m)
    denom = jnp.sum(p, axis=-1, keepdims=True)
    out = jnp.einsum('brhnqk,brhnke->brhnqe', (p / denom).astype(v.dtype), vb).astype(jnp.float32)
    lse = (m + jnp.log(denom))[..., 0]
    out = out.reshape(b, dilation, h, n_blk * w, e)[:, :, :, :n_sub]
    out = out.transpose(0, 3, 1, 2, 4).reshape(b, s, h, e)
    lse = lse.reshape(b, dilation, h, n_blk * w)[..., :n_sub]
    lse = lse.transpose(0, 3, 1, 2).reshape(b, s, h)
    return out, lse


def dilated_attention_mixer(hdn, w_in, q_gain, k_gain, w_out, cos, sin):
    b, s, _ = hdn.shape
    qkv = jnp.einsum('bsd,df->bsf', hdn, w_in).reshape(b, s, 3, N_GROUPS, HEADS_PER_GROUP, HEAD_DIM)
    q = apply_rope(rmsnorm(qkv[:, :, 0], q_gain), cos, sin)
    k = apply_rope(rmsnorm(qkv[:, :, 1], k_gain), cos, sin)
    v = qkv[:, :, 2]
    outs, lses = [], []
    for g, (window, dilation) in enumerate(ATTN_GROUPS):
        o, l = dilated_window_attention(q[:, :, g], k[:, :, g], v[:, :, g], window, dilation)
        outs.append(o)
        lses.append(l)
    out = jnp.stack(outs, axis=0)
    wts = jax.nn.softmax(jnp.stack(lses, axis=0), axis=0)
    mixed = jnp.sum(wts[..., None] * out, axis=0)
    mixed = mixed.reshape(b, s, ATTN_WIDTH).astype(hdn.dtype)
    return jnp.einsum('bsf,fd->bsd', mixed, w_out)


def conformer_conv_mixer(hdn, w_in, b_in, w_dw, b_dw, ln_g, ln_b, w_out, b_out):
    a = jnp.einsum('bsd,df->bsf', hdn, w_in) + b_in
    u = jax.nn.glu(a, axis=-1)
    u = lax.conv_general_dilated(
        u, w_dw[:, None, :].astype(u.dtype), window_strides=(1,),
        padding=((CONV_KERNEL - 1, 0),),
        dimension_numbers=('NWC', 'WIO', 'NWC'),
        feature_group_count=D_MODEL) + b_dw
    u = jax.nn.silu(layernorm(u, ln_g, ln_b))
    return jnp.einsum('bsd,de->bse', u, w_out) + b_out


def squared_relu_mlp(hdn, w1, w2):
    z = jax.nn.relu(jnp.einsum('bsd,df->bsf', hdn, w1))
    return jnp.einsum('bsf,fd->bsd', z * z, w2)


def setup_inputs(seed: int = 0) -> dict:
    key = jax.random.key(seed)
    ks = jax.random.split(key, 18)
    f32 = jnp.float32
    nrm = lambda k, shape, scale: jax.random.normal(k, shape, f32) * scale
    return {
        "x": nrm(ks[0], (BATCH, SEQ, D_MODEL), 1.0),
        "mixer_norm": 1.0 + nrm(ks[1], (DEPTH, D_MODEL), 0.02),
        "mlp_norm": 1.0 + nrm(ks[2], (DEPTH, D_MODEL), 0.02),
        "conv_w_in": nrm(ks[3], (N_CONV_LAYERS, D_MODEL, 2 * D_MODEL), D_MODEL ** -0.5),
        "conv_b_in": nrm(ks[4], (N_CONV_LAYERS, 2 * D_MODEL), 0.01),
        "conv_w_dw": nrm(ks[5], (N_CONV_LAYERS, CONV_KERNEL, D_MODEL), CONV_KERNEL ** -0.5),
        "conv_b_dw": nrm(ks[6], (N_CONV_LAYERS, D_MODEL), 0.01),
        "conv_ln_g": 1.0 + nrm(ks[7], (N_CONV_LAYERS, D_MODEL), 0.02),
        "conv_ln_b": nrm(ks[8], (N_CONV_LAYERS, D_MODEL), 0.01),
        "conv_w_out": nrm(ks[9], (N_CONV_LAYERS, D_MODEL, D_MODEL), D_MODEL ** -0.5),
        "conv_b_out": nrm(ks[10], (N_CONV_LAYERS, D_MODEL), 0.01),
        "attn_w_in": nrm(ks[11], (N_ATTN_LAYERS, D_MODEL, QKV_WIDTH), D_MODEL ** -0.5),
        "attn_q_norm": 1.0 + nrm(ks[12], (N_ATTN_LAYERS, HEAD_DIM), 0.02),
        "attn_k_norm": 1.0 + nrm(ks[13], (N_ATTN_LAYERS, HEAD_DIM), 0.02),
        "attn_w_out": nrm(ks[14], (N_ATTN_LAYERS, ATTN_WIDTH, D_MODEL), ATTN_WIDTH ** -0.5),
        "mlp_w1": nrm(ks[15], (DEPTH, D_MODEL, D_FF), D_MODEL ** -0.5),
        "mlp_w2": nrm(ks[16], (DEPTH, D_FF, D_MODEL), D_FF ** -0.5),
    }


def reference(x, mixer_norm, mlp_norm, conv_w_in, conv_b_in, conv_w_dw, conv_b_dw,
              conv_ln_g, conv_ln_b, conv_w_out, conv_b_out, attn_w_in, attn_q_norm,
              attn_k_norm, attn_w_out, mlp_w1, mlp_w2):
    cos, sin = rope_tables(x.shape[1])
    for i in range(DEPTH):
        hdn = rmsnorm(x, mixer_norm[i])
        j = i // N_MIXERS
        if i % N_MIXERS == 0:
            x = x + conformer_conv_mixer(hdn, conv_w_in[j], conv_b_in[j], conv_w_dw[j], conv_b_dw[j],
                                         conv_ln_g[j], conv_ln_b[j], conv_w_out[j], conv_b_out[j])
        else:
            x = x + dilated_attention_mixer(hdn, attn_w_in[j], attn_q_norm[j], attn_k_norm[j],
                                            attn_w_out[j], cos, sin)
        x = x + squared_relu_mlp(rmsnorm(x, mlp_norm[i]), mlp_w1[i], mlp_w2[i])
    return x
```

```python
import numpy as np
from contextlib import ExitStack
import ml_dtypes
import concourse.bass as bass
import concourse.mybir as mybir
from concourse.bass_utils import run_bass_kernel_spmd

F32 = mybir.dt.float32
BF16 = mybir.dt.bfloat16
AF = mybir.ActivationFunctionType
ALU = mybir.AluOpType

D = 1024
S = 4096
NT = 512
NTL = S // NT
EPS = 1e-6
GROUPS = ((128, 1), (512, 4), (2048, 16))
ARENA_BYTES = 207 * 1024

C_MN = 0
C_LN = 32
C_CBI = 64
C_CWD = 96
C_CBD = 592
C_CLG = 608
C_CLB = 624
C_CBO = 640
C_AQN = 656
C_AKN = 658
NPAR = 660
K_ONES, K_BD, K_ROT, K_ID, K_MNEG, K_MNEG1 = 0, 128, 256, 384, 512, 1024
NCB = 1536


class Buf:
    def __init__(self, name, ap, arena=None, off=0):
        self.name = name
        self.ap = ap
        self.keys = [name]
        self.arena = arena
        self.off = off
        self.pages = ()

    def alias_bf16(self, nbytes):
        b = Buf(self.name, self.arena.t[:, self.off // 2:(self.off + nbytes) // 2], self.arena, self.off)
        b.pages = self.pages
        return b

    def v(self, a):
        return self.ap.rearrange("p (a b) -> p a b", a=a)


class Arena:
    def __init__(self, t, nbytes):
        self.t = t
        self.top = 0
        self.n = nbytes

    def alloc(self, name, nbytes, dtype=BF16):
        off = self.top
        sz = (nbytes + 255) // 256 * 256
        self.top += sz
        assert self.top <= self.n, (name, self.top, self.n)
        ap = self.t[:, off // 2:(off + nbytes) // 2]
        if dtype != BF16:
            ap = ap.bitcast(dtype)
        b = Buf(name, ap, self, off)
        b.pages = tuple(range(off // 256, (off + sz) // 256))
        return b


class Prog:
    def __init__(self, nc, es):
        self.nc = nc
        self.es = es
        self.engs = ("pe", "act", "dve", "pool", "sp")
        self.q = {e: [] for e in self.engs}
        self.csem = {}
        for e in ("pe", "act", "dve", "pool"):
            self.csem[e] = es.enter_context(nc.semaphore("c_" + e))
        self.ccnt = {e: 0 for e in self.csem}
        self.dsem = {}
        self.waited = {e: {} for e in self.engs}
        self.lastw = {}
        self.readers = {}
        self.psn = 0
        self.nops = 0
        self.keypages = {}
        self.prev_pages = {}
        self.cur_pages = {}
        self.epoch = 0
        self._pcache = {}

    def reg(self, key, buf):
        self.keypages[key] = buf.pages

    def new_epoch(self):
        for pg, d in self.cur_pages.items():
            pd = self.prev_pages.setdefault(pg, {})
            for kk, tok in d.items():
                if kk not in pd or pd[kk][1] < tok[1]:
                    pd[kk] = tok
        self.cur_pages = {}
        self.epoch += 1
        self._pcache = {}

    def _pages_of(self, items):
        out = []
        for x in items:
            if isinstance(x, Buf):
                if x.pages:
                    out.append((x.name, x.pages))
            elif x in self.keypages:
                out.append((x, self.keypages[x]))
        return out

    def _page_deps(self, plist):
        toks = {}
        for key, pages in plist:
            c = self._pcache.get(key)
            if c is None:
                c = {}
                for pg in pages:
                    for kk, tok in self.prev_pages.get(pg, {}).items():
                        if kk not in c or c[kk][1] < tok[1]:
                            c[kk] = tok
                self._pcache[key] = c
            for kk, tok in c.items():
                if kk not in toks or toks[kk][1] < tok[1]:
                    toks[kk] = tok
        return toks

    def _page_commit(self, plist, tok):
        kk = id(tok[0])
        for key, pages in plist:
            for pg in pages:
                d = self.cur_pages.setdefault(pg, {})
                if kk not in d or d[kk][1] < tok[1]:
                    d[kk] = tok

    @staticmethod
    def _keys(lst):
        out = []
        for x in lst:
            if hasattr(x, "keys") and not isinstance(x, dict):
                out.extend(x.keys)
            else:
                out.append(x)
        return out

    def _deps(self, e, reads, writes, plist=()):
        deps = {}

        def add(tok, war=False):
            sem, val, src = tok
            if src == "pe" and e == "pe":
                return
            if war and src == e and e in ("act", "dve", "pool"):
                return
            k = id(sem)
            if k not in deps or deps[k][1] < val:
                deps[k] = (sem, val)

        for r in reads:
            t = self.lastw.get(r)
            if t:
                add(t)
        for w in writes:
            t = self.lastw.get(w)
            if t:
                add(t)
            for t in self.readers.get(w, {}).values():
                add(t, war=True)
        if plist and self.prev_pages:
            for t in self._page_deps(plist).values():
                add(t)
        waits = []
        for k, (sem, val) in deps.items():
            if self.waited[e].get(k, 0) >= val:
                continue
            self.waited[e][k] = val
            waits.append((sem, val))
        return waits

    def _commit(self, tok, reads, writes):
        k = id(tok[0])
        for r in reads:
            d = self.readers.setdefault(r, {})
            if k not in d or d[k][1] < tok[1]:
                d[k] = tok
        for w in writes:
            self.lastw[w] = tok
            self.readers[w] = {}

    def op(self, e, fn, reads=(), writes=()):
        plist = self._pages_of(list(reads) + list(writes))
        reads = self._keys(reads)
        writes = self._keys(writes)
        waits = self._deps(e, reads, writes, plist)
        self.ccnt[e] += 1
        tok = (self.csem[e], self.ccnt[e], e)
        self.q[e].append((waits, fn, self.csem[e], False))
        self._commit(tok, reads, writes)
        self._page_commit(plist, tok)
        self.nops += 1

    def dma(self, e, out, in_, key, reads=(), writes=()):
        plist = self._pages_of(list(reads) + list(writes))
        reads = self._keys(reads)
        writes = self._keys(writes)
        waits = self._deps(e, reads, writes, plist)
        if key not in self.dsem:
            self.dsem[key] = [self.es.enter_context(self.nc.semaphore("d%d" % len(self.dsem))), 0]
        ent = self.dsem[key]
        ent[1] += 16
        tok = (ent[0], ent[1], "dma")

        def fn(eng, out=out, in_=in_):
            return eng.dma_start(out=out, in_=in_)

        self.q[e].append((waits, fn, ent[0], True))
        self._commit(tok, reads, writes)
        self._page_commit(plist, tok)
        self.nops += 1

    def barrier(self):
        toks = [(self.csem[e], self.ccnt[e]) for e in self.csem if self.ccnt[e] > 0]
        toks += [(s, c) for (s, c) in self.dsem.values() if c > 0]
        for e in self.engs:
            waits = []
            for sem, val in toks:
                k = id(sem)
                if self.waited[e].get(k, 0) >= val:
                    continue
                self.waited[e][k] = val
                waits.append((sem, val))
            if waits:
                self.q[e].append((waits, None, None, False))

    def replay(self, e, eng):
        for waits, fn, sem, is_dma in self.q[e]:
            for s, v in waits:
                eng.wait_ge(s, v)
            if fn is None:
                continue
            inst = fn(eng)
            inst.then_inc(sem, 16 if is_dma else 1)


class K:
    def __init__(self, nc, P, arena, ps_banks):
        self.nc = nc
        self.P = P
        self.A = arena
        self.ps = ps_banks
        self.psn = 0

    def psum(self):
        b = self.ps[self.psn % getattr(self, "npool", 6)]
        self.psn += 1
        return b

    def mm(self, out, pairs, reads, writes):
        def fn(eng, out=out, pairs=pairs):
            n = len(pairs)
            last = None
            for i, (l, r) in enumerate(pairs):
                last = eng.matmul(out, l, r, start=(i == 0), stop=(i == n - 1))
            return last
        self.P.op("pe", fn, reads, writes)

    def mm1(self, out, l, r, start, stop, reads, writes):
        def fn(eng):
            return eng.matmul(out, l, r, start=start, stop=stop)
        self.P.op("pe", fn, reads, writes)

    def act(self, out, in_, func, reads, writes, bias=None, scale=None):
        def fn(eng):
            kw = {}
            if bias is not None:
                kw["bias"] = bias
            if scale is not None:
                kw["scale"] = scale
            return eng.activation(out=out, in_=in_, func=func, **kw)
        self.P.op("act", fn, reads, writes)

    def stt(self, out, in0, scalar, in1, op0, op1, reads, writes):
        def fn(eng):
            return eng.scalar_tensor_tensor(out=out, in0=in0, scalar=scalar, in1=in1, op0=op0, op1=op1)
        self.P.op("dve", fn, reads, writes)

    def tt(self, e, out, in0, in1, op, reads, writes):
        def fn(eng):
            return eng.tensor_tensor(out=out, in0=in0, in1=in1, op=op)
        self.P.op(e, fn, reads, writes)

    def ts(self, e, out, in0, s1, s2, op0, op1, reads, writes):
        def fn(eng):
            if op1 is None:
                return eng.tensor_scalar(out=out, in0=in0, scalar1=s1, scalar2=None, op0=op0)
            return eng.tensor_scalar(out=out, in0=in0, scalar1=s1, scalar2=s2, op0=op0, op1=op1)
        self.P.op(e, fn, reads, writes)

    def recip(self, out, in_, reads, writes):
        def fn(eng):
            return eng.reciprocal(out=out, in_=in_)
        self.P.op("dve", fn, reads, writes)

    def copy(self, e, out, in_, reads, writes):
        if e == "act":
            def fn(eng):
                return eng.copy(out=out, in_=in_)
        else:
            def fn(eng):
                return eng.tensor_copy(out=out, in_=in_)
        self.P.op(e, fn, reads, writes)

    def memset(self, e, ap, val, writes):
        def fn(eng):
            return eng.memset(ap, val)
        self.P.op(e, fn, (), writes)


def ytile(Y, t):
    return Y.rearrange("(c p) n -> p c n", p=128)[:, :, t * NT:(t + 1) * NT]


def norm_sweep(k, src, srcname, gcol, dst, HT):
    P = k.P
    ones = k.cb.ap[:, K_ONES:K_ONES + 128]

    def load(t):
        xt = k.xt[t % 2]
        P.dma("sp", out=xt.v(8), in_=ytile(src, t), key=("ld", xt.name),
              reads=[(srcname, t)], writes=[xt])

    def comp(t):
        s = t % 2
        xt = k.xt[s]
        ps = k.psum()
        for c in range(8):
            sq = k.sq[c % 2]
            k.act(sq.ap, xt.v(8)[:, c, :], AF.Square, [xt], [sq])
            k.mm1(ps.ap, ones, sq.ap, c == 0, c == 7, [sq, k.cb], [ps])
        st = k.std[s]
        k.act(st.ap, ps.ap, AF.Ln, [ps, k.epsb], [st], bias=k.epsb.ap[:, 0:1], scale=1.0 / D)
        k.act(st.ap, st.ap, AF.Exp, [st], [st], scale=-0.5)
        if dst == "HT":
            ht = k.ht[s]
            for c in range(8):
                k.stt(ht.v(8)[:, c, :], xt.v(8)[:, c, :], k.par.ap[:, gcol + c:gcol + c + 1], st.ap,
                      ALU.mult, ALU.mult, [xt, st, k.par], [ht])
            P.dma("sp", out=ytile(HT, t), in_=ht.v(8), key=("st", ht.name),
                  reads=[ht], writes=[("HT", t)])
        else:
            for c in range(8):
                k.stt(dst.v(8)[:, c, t * NT:(t + 1) * NT], xt.v(8)[:, c, :],
                      k.par.ap[:, gcol + c:gcol + c + 1], st.ap,
                      ALU.mult, ALU.mult, [xt, st, k.par], [("hTall", t)])

    load(0)
    for t in range(NTL):
        if t + 1 < NTL:
            load(t + 1)
        comp(t)


class FusedNorm:
    def __init__(self, k, gcol, HT, staging, sqs, bank):
        self.k, self.gcol, self.HT, self.staging, self.sqs, self.bank = k, gcol, HT, staging, sqs, bank
        self.pend = []
        self.n = 0

    def chunk(self, t, c, xt):
        k = self.k
        sq = self.sqs[self.n % len(self.sqs)]
        self.n += 1
        k.act(sq.ap, xt.v(8)[:, c, :], AF.Square, [xt], [sq])
        self.pend.append((c, sq))
        if len(self.pend) > min(2, len(self.sqs) - 1):
            self._flush1()

    def _flush1(self):
        k = self.k
        c, sq = self.pend.pop(0)
        ones = k.cb.ap[:, K_ONES:K_ONES + 128]
        k.mm1(self.bank.ap, ones, sq.ap, c == 0, c == 7, [sq, k.cb], [self.bank])

    def finish(self, t, xt):
        k = self.k
        while self.pend:
            self._flush1()
        st = k.std[t % 2]
        k.act(st.ap, self.bank.ap, AF.Ln, [self.bank, k.epsb], [st], bias=k.epsb.ap[:, 0:1], scale=1.0 / D)
        k.act(st.ap, st.ap, AF.Exp, [st], [st], scale=-0.5)
        hb = self.staging[t % len(self.staging)]
        for c in range(8):
            k.stt(hb.v(8)[:, c, :], xt.v(8)[:, c, :], k.par.ap[:, self.gcol + c:self.gcol + c + 1], st.ap,
                  ALU.mult, ALU.mult, [xt, st, k.par], [hb])
        k.P.dma("sp", out=ytile(self.HT, t), in_=hb.v(8), key=("st", "fn", t % len(self.staging)),
                reads=[hb], writes=[("HT", t)])


def mlp_layer(k, l, Y, HT, W1d, W2d, prenormed=False, next_gcol=None):
    P = k.P
    A = k.A
    mark = A.top
    k.npool = 8
    k.ht = [A.alloc("mlp.ht%d" % i, 8192) for i in range(2)]
    w1 = [A.alloc("mlp.w1_%d" % i, 16384) for i in range(2)]
    w2 = [A.alloc("mlp.w2_%d" % i, 16384) for i in range(2)]
    z = [A.alloc("mlp.z%d" % i, 8192) for i in range(2)]
    sqv = [A.alloc("mlp.sqv%d" % i, 2048, F32) for i in range(3)]
    fnorm = None
    if next_gcol is not None:
        k.npool = 7
        hn = [A.alloc("mlp.hn%d" % i, 8192) for i in range(2)]
        fsq = [A.alloc("mlp.fsq%d" % i, 1024) for i in range(4)]
        fnorm = FusedNorm(k, next_gcol, HT, hn, fsq, k.ps[7])

    def load_w(q):
        s = q % 2
        src1 = W1d[l].rearrange("(c p) f -> p c f", p=128)[:, :, q * 1024:(q + 1) * 1024]
        src2 = W2d[l, q * 1024:(q + 1) * 1024, :].rearrange("(j p) d -> p j d", p=128)
        for h in range(2):
            P.dma("pool", out=w1[s].v(8)[:, 4 * h:4 * h + 4, :], in_=src1[:, 4 * h:4 * h + 4, :],
                  key=("ld", w1[s].name, h), reads=[], writes=[(w1[s].name, h)])
        for h in range(2):
            P.dma("pool", out=w2[s].v(8)[:, 4 * h:4 * h + 4, :], in_=src2[:, 4 * h:4 * h + 4, :],
                  key=("ld", w2[s].name, h), reads=[], writes=[(w2[s].name, h)])

    def wk(b):
        return [(b.name, 0), (b.name, 1)]
    for b_ in w1 + w2:
        for h_ in range(2):
            P.reg((b_.name, h_), b_)

    load_w(0)
    if not prenormed:
        norm_sweep(k, Y, "Y", C_LN + 8 * l, "HT", HT)
    items = [(q, t) for q in range(4) for t in range(NTL)]
    cnt = [0]

    def LH(i):
        q, t = items[i]
        s = i % 2
        P.dma("sp", out=k.ht[s].v(8), in_=ytile(HT, t), key=("ld", k.ht[s].name),
              reads=[("HT", t)], writes=[k.ht[s]])

    def LX(i):
        q, t = items[i]
        s = i % 2
        P.dma("sp", out=k.xt[s].v(8), in_=ytile(Y, t), key=("ld", k.xt[s].name),
              reads=[("Y", t)], writes=[k.xt[s]])

    def P1(i):
        q, t = items[i]
        s = i % 2
        ws = q % 2
        ht, zz = k.ht[s], z[s]
        for j in range(8):
            ps = k.psum()
            k.mm(ps.ap, [(w1[ws].v(8)[:, kc, j * 128:(j + 1) * 128], ht.v(8)[:, kc, :]) for kc in range(8)],
                 wk(w1[ws]) + [ht], [ps])
            sv = sqv[cnt[0] % 3]
            cnt[0] += 1
            k.act(sv.ap, ps.ap, AF.Square, [ps], [sv])
            k.stt(zz.v(8)[:, j, :], ps.ap, 0.0, sv.ap, ALU.is_gt, ALU.mult, [ps, sv], [zz])

    def P2(i):
        q, t = items[i]
        s = i % 2
        ws = q % 2
        xt, zz = k.xt[s], z[s]
        for o in range(8):
            ps = k.psum()
            k.mm(ps.ap, [(w2[ws].v(8)[:, j, o * 128:(o + 1) * 128], zz.v(8)[:, j, :]) for j in range(8)],
                 wk(w2[ws]) + [zz], [ps])
            k.tt("dve", xt.v(8)[:, o, :], xt.v(8)[:, o, :], ps.ap, ALU.add, [xt, ps], [xt])
            if fnorm is not None and q == 3:
                fnorm.chunk(t, o, xt)
        P.dma("sp", out=ytile(Y, t), in_=xt.v(8), key=("st", xt.name),
              reads=[xt], writes=[("Y", t)])
        if fnorm is not None and q == 3:
            fnorm.finish(t, xt)

    LH(0)
    LX(0)
    LH(1)
    P1(0)
    for i in range(len(items)):
        q, t = items[i]
        if t == (2 if q == 0 else 0) and q + 1 < 4:
            load_w(q + 1)
        if i + 1 < len(items):
            LX(i + 1)
            P1(i + 1)
        if i + 2 < len(items):
            LH(i + 2)
        P2(i)
    A.top = mark


def conv_layer(k, l, j, X, xname, Y, HT, Wi, Wo, prenormed=False, next_gcol=None):
    P = k.P
    A = k.A
    mark = A.top
    k.npool = 6
    k.ht = [A.alloc("cv.ht%d" % i, 8192) for i in range(2)]
    w_in = A.alloc("cv.w_in", 32768)
    w_out = A.alloc("cv.w_out", 16384)
    diag = A.alloc("cv.diag", 31 * 8 * 256)
    u = [A.alloc("cv.u%d" % i, 8 * 544 * 2) for i in range(2)]
    cbf = [A.alloc("cv.cbf%d" % i, 1024) for i in range(2)]
    csq = [A.alloc("cv.csq%d" % i, 1024) for i in range(2)]
    sg = [A.alloc("cv.sg%d" % i, 2048, F32) for i in range(2)]
    mean = A.alloc("cv.mean", 2048, F32)
    var = A.alloc("cv.var", 2048, F32)
    nmr = A.alloc("cv.nmr", 2048, F32)
    y1 = [A.alloc("cv.y1_%d" % i, 2048, F32) for i in range(2)]
    zb = [k.std[0], k.std[1]]
    cc = k.xt[1]
    xt = k.xt[0]

    w_in_k = [(w_in.name, h) for h in range(4)]
    w_out_k = [(w_out.name, h) for h in range(2)]
    for kk_ in w_in_k:
        P.reg(kk_, w_in)
    for kk_ in w_out_k:
        P.reg(kk_, w_out)
    srci = Wi[j].rearrange("(c p) f -> p c f", p=128)
    srco = Wo[j].rearrange("(c p) f -> p c f", p=128)

    def ld_in(h):
        P.dma("pool", out=w_in.v(8)[:, 2 * h:2 * h + 2, :], in_=srci[:, 2 * h:2 * h + 2, :],
              key=("ld", w_in.name, h), reads=[], writes=[(w_in.name, h)])

    def ld_out(h):
        P.dma("pool", out=w_out.v(8)[:, 4 * h:4 * h + 4, :], in_=srco[:, 4 * h:4 * h + 4, :],
              key=("ld", w_out.name, h), reads=[], writes=[(w_out.name, h)])
    ld_in(0)
    ld_in(1)
    ld_out(0)
    ld_out(1)
    ld_in(2)
    ld_in(3)
    for n_ in range(248):
        P.reg(("diag", n_), diag)
    ident = k.cb.ap[:, K_ID:K_ID + 128]
    ones = k.cb.ap[:, K_ONES:K_ONES + 128]
    dv = diag.v(248)
    n = 0
    for kk in range(31):
        for c in range(8):
            col = C_CWD + 248 * j + kk * 8 + c
            if n % 2 == 0:
                k.ts("dve", dv[:, kk * 8 + c, :], ident, k.par.ap[:, col:col + 1], None, ALU.mult, None,
                     [k.cb, k.par], [("diag", n)])
            else:
                k.act(dv[:, kk * 8 + c, :], ident, AF.Copy, [k.cb, k.par], [("diag", n)],
                      scale=k.par.ap[:, col:col + 1])
            n += 1
    diag_k = [("diag", n - 1), ("diag", n - 2)]

    if not prenormed:
        norm_sweep(k, X, xname, C_MN + 8 * l, "HT", HT)
    fnorm = None
    k.npool = 5
    fsq = [A.alloc("cv.fsq%d" % i, 1024) for i in range(2)]
    if next_gcol is not None:
        fnorm = FusedNorm(k, next_gcol, HT, [u[0].alias_bf16(8192), u[1].alias_bf16(8192)], fsq, k.ps[5])

    uv = [b.v(8) for b in u]
    bcol = C_CBI + 16 * j

    def L(t):
        ht = k.ht[t % 2]
        P.dma("sp", out=ht.v(8), in_=ytile(HT, t), key=("ld", ht.name),
              reads=[("HT", t)], writes=[ht])

    def LX(t):
        P.dma("sp", out=xt.v(8), in_=ytile(X, t), key=("ld", xt.name),
              reads=[(xname, t)], writes=[xt])

    def SA0(t):
        s = t % 2
        if t == 0:
            k.memset("pool", uv[s][:, :, 0:30], 0.0, [u[s]])
        else:
            k.copy("pool", uv[s][:, :, 0:30], uv[1 - s][:, :, 512:542], [u[1 - s]], [u[s]])

    def SAc(t, c):
        s = t % 2
        ht = k.ht[s]
        if True:
            psv = k.psum()
            k.mm(psv.ap, [(w_in.v(8)[:, kc, c * 128:(c + 1) * 128], ht.v(8)[:, kc, :]) for kc in range(8)],
                 w_in_k + [ht], [psv])
            psg = k.psum()
            k.mm(psg.ap, [(w_in.v(8)[:, kc, 1024 + c * 128:1024 + (c + 1) * 128], ht.v(8)[:, kc, :]) for kc in range(8)],
                 w_in_k + [ht], [psg])
            g = sg[c % 2]
            k.act(g.ap, psg.ap, AF.Sigmoid, [psg, k.par], [g], bias=k.par.ap[:, bcol + 8 + c:bcol + 9 + c])
            k.stt(uv[s][:, c, 30:542], psv.ap, k.par.ap[:, bcol + c:bcol + c + 1], g.ap, ALU.add, ALU.mult,
                  [psv, g, k.par], [u[s]])

    stats_pend = []

    def stats_flush(n_keep):
        while len(stats_pend) > n_keep:
            c, b1, b2 = stats_pend.pop(0)
            k.mm1(k.ps[6].ap, ones, b1.ap, c == 0, c == 7, [b1, k.cb], [k.ps[6]])
            k.mm1(k.ps[7].ap, ones, b2.ap, c == 0, c == 7, [b2, k.cb], [k.ps[7]])

    def SBc(t, c):
        s = t % 2
        ps = k.psum()
        k.mm(ps.ap, [(dv[:, kk * 8 + c, :], uv[s][:, c, kk:kk + 512]) for kk in range(31)],
             diag_k + [u[s]], [ps])
        stats_flush(1)
        bc = C_CBD + 8 * j + c
        k.ts("dve", cc.v(8)[:, c, :], ps.ap, k.par.ap[:, bc:bc + 1], None, ALU.add, None,
             [ps, k.par], [cc])
        b1, b2 = cbf[c % 2], csq[c % 2]
        k.act(b1.ap, cc.v(8)[:, c, :], AF.Copy, [cc], [b1])
        k.act(b2.ap, cc.v(8)[:, c, :], AF.Square, [cc], [b2])
        stats_pend.append((c, b1, b2))

    def SC0(t):
        ps_sum = k.ps[6]
        ps_sq = k.ps[7]
        k.ts("dve", mean.ap, ps_sum.ap, 1.0 / D, None, ALU.mult, None, [ps_sum], [mean])
        k.tt("dve", var.ap, mean.ap, mean.ap, ALU.mult, [mean], [var])
        k.stt(var.ap, ps_sq.ap, 1.0 / D, var.ap, ALU.mult, ALU.subtract, [ps_sq, var], [var])
        k.act(var.ap, var.ap, AF.Ln, [var, k.epsb], [var], bias=k.epsb.ap[:, 0:1])
        k.act(var.ap, var.ap, AF.Exp, [var], [var], scale=-0.5)
        k.tt("dve", nmr.ap, mean.ap, var.ap, ALU.mult, [mean, var], [nmr])

    def SCc(t, c):
        sb = k.ht[t % 2]
        if True:
            yy = y1[c % 2]
            k.tt("dve", yy.ap, cc.v(8)[:, c, :], var.ap, ALU.mult, [cc, var], [yy])
            k.tt("dve", yy.ap, yy.ap, nmr.ap, ALU.subtract, [yy, nmr], [yy])
            gc = C_CLG + 8 * j + c
            bc = C_CLB + 8 * j + c
            zz = zb[c % 2]
            k.act(zz.ap, yy.ap, AF.Sigmoid, [yy, k.par], [zz],
                  bias=k.par.ap[:, bc:bc + 1], scale=k.par.ap[:, gc:gc + 1])
            k.ts("dve", yy.ap, yy.ap, k.par.ap[:, gc:gc + 1], k.par.ap[:, bc:bc + 1], ALU.mult, ALU.add,
                 [yy, k.par], [yy])
            k.tt("dve", sb.v(8)[:, c, :], yy.ap, zz.ap, ALU.mult, [yy, zz], [sb])

    def SD(t):
        sb = k.ht[t % 2]
        for o in range(8):
            ps = k.psum()
            k.mm(ps.ap, [(w_out.v(8)[:, c, o * 128:(o + 1) * 128], sb.v(8)[:, c, :]) for c in range(8)],
                 w_out_k + [sb], [ps])
            bc = C_CBO + 8 * j + o
            k.stt(xt.v(8)[:, o, :], ps.ap, k.par.ap[:, bc:bc + 1], xt.v(8)[:, o, :], ALU.add, ALU.add,
                  [ps, xt, k.par], [xt])
            if fnorm is not None:
                fnorm.chunk(t, o, xt)
        P.dma("sp", out=ytile(Y, t), in_=xt.v(8), key=("st", xt.name),
              reads=[xt], writes=[("Y", t)])
        if fnorm is not None:
            fnorm.finish(t, xt)

    NPRE = 3
    L(0)
    SA0(0)
    for c in range(8):
        SAc(0, c)
    for c in range(NPRE):
        SBc(0, c)
    for t in range(NTL):
        if t + 1 < NTL:
            L(t + 1)
        LX(t)
        for c in range(NPRE, 8):
            SBc(t, c)
        stats_flush(0)
        SC0(t)
        if t + 1 < NTL:
            SA0(t + 1)
        for c in range(8):
            if t + 1 < NTL:
                SAc(t + 1, c)
            SCc(t, c)
        if t + 1 < NTL:
            for c in range(NPRE):
                SBc(t + 1, c)
        SD(t)
    A.top = mark


def attn_layer(k, l, j, Y, HT, OTs, Win, Wout, COSd, SINd, prenormed=False, next_gcol=None):
    P = k.P
    A = k.A
    mark = A.top
    k.npool = 8
    hT = A.alloc("at.hTall", 65536)
    wq = [A.alloc("at.wq%d" % i, 2048) for i in range(2)]
    wk = [A.alloc("at.wk%d" % i, 2048) for i in range(2)]
    wv = [A.alloc("at.wv%d" % i, 2048) for i in range(2)]
    QT = A.alloc("at.QT", 8192)
    KT = A.alloc("at.KT", 8192)
    VA = A.alloc("at.VA", 16384)
    cosb = [A.alloc("at.cos%d" % i, 2048, F32) for i in range(2)]
    sinb = [A.alloc("at.sin%d" % i, 2048, F32) for i in range(2)]
    sq3 = [A.alloc("at.sq%d" % i, 1024) for i in range(3)]
    rs = [A.alloc("at.rs%d" % i, 2048, F32) for i in range(2)]
    qn = [A.alloc("at.qn%d" % i, 1024) for i in range(3)]
    t1 = [A.alloc("at.t1_%d" % i, 2048, F32) for i in range(2)]
    t2 = [A.alloc("at.t2_%d" % i, 2048, F32) for i in range(2)]
    PTb = [A.alloc("at.PT%d" % i, 1024) for i in range(4)]
    DEN = [A.alloc("at.den%d" % i, 2048, F32) for i in range(2)]
    otst = [A.alloc("at.otst%d" % i, 1024) for i in range(2)]
    otl = [A.alloc("at.otl%d" % i, 4096) for i in range(2)]
    w_out = A.alloc("at.w_out", 8192)
    ACC = [k.xt[0], k.xt[1]]

    bd = k.cb.ap[:, K_BD:K_BD + 128]
    rot = k.cb.ap[:, K_ROT:K_ROT + 128]
    ident = k.cb.ap[:, K_ID:K_ID + 128]
    mneg = k.cb.ap[:, K_MNEG:K_MNEG + 512]
    mneg1 = k.cb.ap[:, K_MNEG1:K_MNEG1 + 512]
    hv = hT.v(8)
    allh = [("hTall", t) for t in range(NTL)]
    for kk_ in allh:
        P.reg(kk_, hT)
    for b0_ in range(0, 32, 4):
        P.reg(("VA", b0_), VA)

    src = Wout[j].rearrange("(c p) d -> p c d", p=128)
    P.dma("pool", out=w_out.v(4), in_=src, key=("ld", w_out.name), reads=[], writes=[w_out])
    wsrc = Win[j].rearrange("(c p) f -> p c f", p=128)

    def load_w(i, hp, g):
        s = i % 2
        for which, wb in enumerate((wq[s], wk[s], wv[s])):
            off = which * 1536 + g * 512 + hp * 128
            P.dma("pool", out=wb.v(8), in_=wsrc[:, :, off:off + 128], key=("ld", wb.name),
                  reads=[], writes=[wb])

    sweeps = [(hp, g) for hp in range(4) for g in (2, 1, 0)]
    pending_nz = []
    load_w(0, *sweeps[0])
    if prenormed:
        for t in range(NTL):
            P.dma("sp", out=hT.v(8)[:, :, t * NT:(t + 1) * NT], in_=ytile(HT, t), key=("ld", "hTall", t % 2),
                  reads=[("HT", t)], writes=[("hTall", t)])
    else:
        norm_sweep(k, Y, "Y", C_MN + 8 * l, hT, None)
    VAv = VA.v(32)
    k.memset("pool", VAv[:, :, 64:192], 1.0, [VA])
    OTd = OTs.rearrange("(c p) n -> p c n", p=128)

    for i, (hp, g) in enumerate(sweeps):
        if i + 1 < len(sweeps):
            load_w(i + 1, *sweeps[i + 1])
        ws = i % 2
        d = GROUPS[g][1]
        nsub = S // d
        nb = nsub // 128

        def VG(b0):
            ps = k.psum()
            for bi in range(4):
                b = b0 + bi
                r, qb = b // nb, b % nb
                st_ = r + d * 128 * qb
                k.mm(ps.ap[:, bi * 128:(bi + 1) * 128],
                     [(hv[:, kc, st_:st_ + 127 * d + 1:d], wv[ws].v(8)[:, kc, :]) for kc in range(8)],
                     [wv[ws]] + allh, [ps])
            psv = ps.ap.rearrange("p (b c) -> p b c", b=4)
            k.copy("act", VAv[:, b0:b0 + 4, 0:64], psv[:, :, 0:64], [ps], [("VA", b0)])
            k.copy("dve", VAv[:, b0:b0 + 4, 192:256], psv[:, :, 64:128], [ps], [("VA", b0)])

        pitems = [(t, w) for t in range(NTL) for w in range(2)]
        pst = {}

        def PA(n):
            t, w = pitems[n]
            s = t % 2
            if w == 0:
                P.dma("sp", out=cosb[s].ap, in_=COSd[:, t * NT:(t + 1) * NT], key=("ld", cosb[s].name),
                      reads=[], writes=[cosb[s]])
                P.dma("sp", out=sinb[s].ap, in_=SINd[:, t * NT:(t + 1) * NT], key=("ld", sinb[s].name),
                      reads=[], writes=[sinb[s]])
            wb = (wq[ws], wk[ws])[w]
            ps = k.psum()
            k.mm(ps.ap, [(wb.v(8)[:, kc, :], hv[:, kc, t * NT:(t + 1) * NT]) for kc in range(8)],
                 [wb, ("hTall", t)], [ps])
            sq = sq3[n % 3]
            k.act(sq.ap, ps.ap, AF.Square, [ps], [sq])
            pst[n] = ps

        def PB(n):
            t, w = pitems[n]
            gcol = (C_AQN + j, C_AKN + j)[w]
            ps = pst[n]
            sq = sq3[n % 3]
            ps2 = k.psum()
            k.mm1(ps2.ap, bd, sq.ap, True, True, [sq, k.cb], [ps2])
            r_ = rs[n % 2]
            k.act(r_.ap, ps2.ap, AF.Ln, [ps2, k.epsb], [r_], bias=k.epsb.ap[:, 0:1], scale=1.0 / 64)
            k.act(r_.ap, r_.ap, AF.Exp, [r_], [r_], scale=-0.5)
            q_ = qn[n % 3]
            k.stt(q_.ap, ps.ap, k.par.ap[:, gcol:gcol + 1], r_.ap, ALU.mult, ALU.mult,
                  [ps, r_, k.par], [q_])

        def PC(n):
            t, w = pitems[n]
            s = t % 2
            dstb = (QT, KT)[w]
            q_ = qn[n % 3]
            ps3 = k.psum()
            k.mm1(ps3.ap, rot, q_.ap, True, True, [q_, k.cb], [ps3])
            a_, b_ = t1[n % 2], t2[n % 2]
            k.tt("pool", a_.ap, q_.ap, cosb[s].ap, ALU.mult, [q_, cosb[s]], [a_])
            k.tt("dve", b_.ap, ps3.ap, sinb[s].ap, ALU.mult, [ps3, sinb[s]], [b_])
            l0 = t * NT // d
            ln = NT // d
            if d == 1:
                dst = dstb.ap[:, t * NT:(t + 1) * NT]
                av, bv = a_.ap, b_.ap
            else:
                dst = dstb.ap.rearrange("p (r l) -> p r l", r=d)[:, :, l0:l0 + ln]
                av = a_.ap.rearrange("p (l r) -> p r l", r=d)
                bv = b_.ap.rearrange("p (l r) -> p r l", r=d)
            k.tt("pool" if w == 0 else "dve", dst, av, bv, ALU.add, [a_, b_], [dstb])

        npi = len(pitems)
        for n in range(npi + 2):
            if n < npi:
                PA(n)
            if n % 2 == 1 and pending_nz:
                pending_nz.pop(0)()
            if n % 2 == 0 and n // 2 < 8:
                VG(4 * (n // 2))
            if 0 <= n - 1 < npi:
                PB(n - 1)
            if 0 <= n - 2 < npi:
                PC(n - 2)
        VAk = [VA] + [("VA", b0) for b0 in range(0, 32, 4)]

        aitems = [(h, r, qb) for r in range(d) for qb in range(0, nb, 2) for h in range(2)]
        ast = {}

        def AS(n):
            h, r, qb = aitems[n]
            rows = slice(0, 64) if h == 0 else slice(64, 128)
            b0 = r * nb + qb
            b1 = b0 + 1
            pss = k.psum()
            mlist = [(pss.ap, ident, (mneg if qb > 0 else mneg1))]
            if qb > 0:
                mlist.append((pss.ap[:, 0:128], KT.ap[rows, (b0 - 1) * 128:b0 * 128], QT.ap[rows, b0 * 128:(b0 + 1) * 128]))
            mlist.append((pss.ap[:, 128:256], KT.ap[rows, b0 * 128:(b0 + 1) * 128], QT.ap[rows, b0 * 128:(b0 + 1) * 128]))
            mlist.append((pss.ap[:, 256:384], KT.ap[rows, b0 * 128:(b0 + 1) * 128], QT.ap[rows, b1 * 128:(b1 + 1) * 128]))
            mlist.append((pss.ap[:, 384:512], KT.ap[rows, b1 * 128:(b1 + 1) * 128], QT.ap[rows, b1 * 128:(b1 + 1) * 128]))

            def fn(eng, mlist=mlist):
                last = None
                nn = len(mlist)
                for ii, (o_, l_, r_) in enumerate(mlist):
                    last = eng.matmul(o_, l_, r_, start=(ii == 0), stop=(ii == nn - 1))
                return last
            P.op("pe", fn, [KT, QT, k.cb], [pss])
            PT = PTb[n % 4]
            k.act(PT.ap, pss.ap, AF.Exp, [pss], [PT], scale=0.125)

        def AV(n):
            h, r, qb = aitems[n]
            vsel = slice(0, 128) if h == 0 else slice(128, 256)
            acc = ACC[h]
            b0 = r * nb + qb
            b1 = b0 + 1
            PT = PTb[n % 4]
            pso = k.psum()
            if qb > 0:
                k.mm(pso.ap[:, 0:128], [(VAv[:, b0 - 1, vsel], PT.ap[:, 0:128]),
                                        (VAv[:, b0, vsel], PT.ap[:, 128:256])], VAk + [PT], [pso])
            else:
                k.mm(pso.ap[:, 0:128], [(VAv[:, b0, vsel], PT.ap[:, 128:256])], VAk + [PT], [pso])
            k.mm(pso.ap[:, 128:256], [(VAv[:, b0, vsel], PT.ap[:, 256:384]),
                                      (VAv[:, b1, vsel], PT.ap[:, 384:512])], VAk + [PT], [pso])
            st_ = r + d * 128 * qb
            accv = acc.ap[:, st_:st_ + 255 * d + 1:d]
            if g == 2:
                k.copy("act", accv, pso.ap[:, 0:256], [pso], [acc])
            else:
                k.tt("dve", accv, accv, pso.ap[:, 0:256], ALU.add, [acc, pso], [acc])

        nai = len(aitems)
        SK = 2
        for n in range(nai + SK):
            if n < nai:
                AS(n)
            if 0 <= n - SK < nai:
                AV(n - SK)
        if g != 0:
            continue
        def make_nz(t, hp=hp):
            def nz():
                s = t % 2
                cs = slice(t * NT, (t + 1) * NT)
                dn = DEN[s]
                P.dma("sp", out=dn.ap[0:64, :], in_=ACC[0].ap[64:128, cs], key=("ld", dn.name),
                      reads=[ACC[0]], writes=[dn])
                P.dma("sp", out=dn.ap[64:128, :], in_=ACC[1].ap[0:64, cs], key=("ld", dn.name),
                      reads=[ACC[1]], writes=[dn])
                k.act(dn.ap, dn.ap, AF.Ln, [dn], [dn])
                k.act(dn.ap, dn.ap, AF.Exp, [dn], [dn], scale=-1.0)
                ot = otst[s]
                k.tt("dve", ot.ap[0:64, :], ACC[0].ap[0:64, cs], dn.ap[0:64, :], ALU.mult, [ACC[0], dn], [ot])
                k.tt("pool", ot.ap[64:128, :], ACC[1].ap[64:128, cs], dn.ap[64:128, :], ALU.mult,
                     [ACC[1], dn], [ot])
                P.dma("sp", out=OTd[:, hp, cs], in_=ot.ap, key=("st", ot.name), reads=[ot], writes=[("OT", t)])
            return nz
        pending_nz.extend(make_nz(t) for t in range(NTL))
    while pending_nz:
        pending_nz.pop(0)()

    def LO(t):
        s = t % 2
        P.dma("sp", out=otl[s].v(4), in_=OTd[:, 0:4, t * NT:(t + 1) * NT], key=("ld", otl[s].name),
              reads=[("OT", t)], writes=[otl[s]])
        P.dma("sp", out=k.xt[s].v(8), in_=ytile(Y, t), key=("ld", k.xt[s].name),
              reads=[("Y", t)], writes=[k.xt[s]])

    fnorm = None
    if next_gcol is not None:
        k.npool = 7
        fnorm = FusedNorm(k, next_gcol, HT, [QT, KT], sq3 + qn, k.ps[7])
    LO(0)
    for t in range(NTL):
        s = t % 2
        if t + 1 < NTL:
            LO(t + 1)
        xt = k.xt[s]
        for o in range(8):
            ps = k.psum()
            k.mm(ps.ap, [(w_out.v(4)[:, c, o * 128:(o + 1) * 128], otl[s].v(4)[:, c, :]) for c in range(4)],
                 [w_out, otl[s]], [ps])
            k.tt("dve", xt.v(8)[:, o, :], xt.v(8)[:, o, :], ps.ap, ALU.add, [xt, ps], [xt])
            if fnorm is not None:
                fnorm.chunk(t, o, xt)
        P.dma("sp", out=ytile(Y, t), in_=xt.v(8), key=("st", xt.name), reads=[xt], writes=[("Y", t)])
        if fnorm is not None:
            fnorm.finish(t, xt)
    A.top = mark


def build(stages, debug=False):
    nc = bass.Bass("TRN2", target_bir_lowering=False)

    def dten(name, shape, dtype, kind):
        return nc.dram_tensor(name, shape, dtype, kind=kind).ap()

    X = dten("x", [D, S], F32, "ExternalInput")
    Y = dten("y", [D, S], F32, "ExternalOutput")
    HT = dten("ht_scr", [D, S], BF16, "ExternalOutput" if debug else "Internal")
    OTs = dten("ot_scr", [512, S], BF16, "Internal")
    PARd = dten("par", [128, NPAR], F32, "ExternalInput")
    CBd = dten("cb", [128, NCB], BF16, "ExternalInput")
    COSd = dten("cos", [128, S], F32, "ExternalInput")
    SINd = dten("sin", [128, S], F32, "ExternalInput")
    Wi = dten("conv_w_in", [2, D, 2 * D], F32, "ExternalInput")
    Wo = dten("conv_w_out", [2, D, D], F32, "ExternalInput")
    Win = dten("attn_w_in", [2, D, 4608], F32, "ExternalInput")
    Wout = dten("attn_w_out", [2, 512, D], F32, "ExternalInput")
    W1d = dten("mlp_w1", [4, D, 4 * D], F32, "ExternalInput")
    W2d = dten("mlp_w2", [4, 4 * D, D], F32, "ExternalInput")

    with ExitStack() as es:
        arena_t = es.enter_context(nc.sbuf_tensor("arena", [128, ARENA_BYTES // 2], BF16))
        banks = []
        for i in range(8):
            pt = es.enter_context(nc.psum_tensor("ps%d" % i, [128, 512], F32))
            banks.append(Buf(("ps", i), pt[:]))
        P = Prog(nc, es)
        A = Arena(arena_t, ARENA_BYTES)
        k = K(nc, P, A, banks)
        k.dbg = debug if isinstance(debug, str) else None
        k.par = A.alloc("par", NPAR * 4, F32)
        k.cb = A.alloc("cb", NCB * 2)
        k.epsb = A.alloc("epsb", 4, F32)
        k.xt = [A.alloc("xt%d" % i, 16384, F32) for i in range(2)]
        k.std = [A.alloc("std%d" % i, 2048, F32) for i in range(2)]
        k.sq = [A.alloc("sq%d" % i, 1024) for i in range(2)]
        P.dma("sp", out=k.par.ap, in_=PARd, key=("ld", "par"), reads=[], writes=[k.par])
        P.dma("sp", out=k.cb.ap, in_=CBd, key=("ld", "cb"), reads=[], writes=[k.cb])
        k.memset("dve", k.epsb.ap, EPS, [k.epsb])

        first = True
        for si, st in enumerate(stages):
            nxt = stages[si + 1] if si + 1 < len(stages) else None
            ng = None
            if nxt is not None:
                ng = (C_LN if nxt[0] == "mlp" else C_MN) + 8 * nxt[1]
            pn = not first
            if st[0] == "conv":
                _, l, j = st
                conv_layer(k, l, j, X if first else Y, "X" if first else "Y", Y, HT, Wi, Wo,
                           prenormed=pn, next_gcol=ng)
            elif st[0] == "attn":
                _, l, j = st
                assert not first
                attn_layer(k, l, j, Y, HT, OTs, Win, Wout, COSd, SINd, prenormed=pn, next_gcol=ng)
            elif st[0] == "mlp":
                _, l = st
                assert not first
                mlp_layer(k, l, Y, HT, W1d, W2d, prenormed=pn, next_gcol=ng)
            first = False
            P.new_epoch()
        P.barrier()

        with nc.Block() as block:
            @block.sync
            def _(eng):
                P.replay("sp", eng)

            @block.tensor
            def _(eng):
                P.replay("pe", eng)

            @block.scalar
            def _(eng):
                P.replay("act", eng)

            @block.vector
            def _(eng):
                P.replay("dve", eng)

            @block.gpsimd
            def _(eng):
                P.replay("pool", eng)
    return nc


ALL_STAGES = []
for _l in range(4):
    ALL_STAGES.append(("conv", _l, _l // 2) if _l % 2 == 0 else ("attn", _l, _l // 2))
    ALL_STAGES.append(("mlp", _l))


def vec8(v):
    return np.ascontiguousarray(np.asarray(v, np.float32).reshape(8, 128).T)


def host_consts(inp):
    par = np.zeros((128, NPAR), np.float32)
    for l in range(4):
        par[:, C_MN + 8 * l:C_MN + 8 * l + 8] = vec8(inp["mixer_norm"][l])
        par[:, C_LN + 8 * l:C_LN + 8 * l + 8] = vec8(inp["mlp_norm"][l])
    for j in range(2):
        par[:, C_CBI + 16 * j:C_CBI + 16 * j + 16] = np.asarray(inp["conv_b_in"][j], np.float32).reshape(16, 128).T
        w = np.asarray(inp["conv_w_dw"][j], np.float32)
        par[:, C_CWD + 248 * j:C_CWD + 248 * (j + 1)] = w.reshape(31, 8, 128).transpose(2, 0, 1).reshape(128, 248)
        par[:, C_CBD + 8 * j:C_CBD + 8 * j + 8] = vec8(inp["conv_b_dw"][j])
        par[:, C_CLG + 8 * j:C_CLG + 8 * j + 8] = vec8(inp["conv_ln_g"][j])
        par[:, C_CLB + 8 * j:C_CLB + 8 * j + 8] = vec8(inp["conv_ln_b"][j])
        par[:, C_CBO + 8 * j:C_CBO + 8 * j + 8] = vec8(inp["conv_b_out"][j])
        par[:, C_AQN + j] = np.tile(np.asarray(inp["attn_q_norm"][j], np.float32), 2)
        par[:, C_AKN + j] = np.tile(np.asarray(inp["attn_k_norm"][j], np.float32), 2)
    cb = np.zeros((128, NCB), np.float32)
    cb[:, K_ONES:K_ONES + 128] = 1.0
    p = np.arange(128)
    cb[:, K_BD:K_BD + 128] = (p[:, None] // 64 == p[None, :] // 64)
    partner = np.where(p % 64 < 32, p + 32, p - 32)
    cb[:, K_ROT:K_ROT + 128] = (p[:, None] == partner[None, :])
    cb[:, K_ID:K_ID + 128] = np.eye(128)
    mprev = (p[:, None] >= p[None, :])
    mcur = (p[:, None] <= p[None, :])
    NEG = -30000.0
    mp = np.where(mprev, 0.0, NEG)
    mc = np.where(mcur, 0.0, NEG)
    cb[:, K_MNEG:K_MNEG + 512] = np.concatenate([mp, mc, mp, mc], axis=1)
    cb[:, K_MNEG1:K_MNEG1 + 512] = np.concatenate([np.full((128, 128), NEG), mc, mp, mc], axis=1)
    pos = np.arange(S, dtype=np.float32)
    inv = (np.float32(10000.0) ** (-np.arange(0, 64, 2, dtype=np.float32) / np.float32(64))).astype(np.float32)
    ang = (pos[None, :] * inv[:, None]).astype(np.float32)
    cosr = np.cos(ang).astype(np.float32)
    sinr = np.sin(ang).astype(np.float32)
    idx = p % 32
    sign = np.where(p % 64 < 32, -1.0, 1.0).astype(np.float32)
    cos_t = np.ascontiguousarray(cosr[idx])
    sin_t = np.ascontiguousarray(sinr[idx] * sign[:, None])
    return par, cb.astype(ml_dtypes.bfloat16), cos_t, sin_t


_NC_CACHE = {}


def run(inputs, stages, debug=False):
    inp = {k_: np.asarray(v) for k_, v in inputs.items()}
    key = tuple(stages)
    if key not in _NC_CACHE:
        _NC_CACHE[key] = build(stages, debug)
    nc = _NC_CACHE[key]
    par, cb, cos_t, sin_t = host_consts(inp)
    x = np.asarray(inp["x"], np.float32)
    shared = {
        "par": par, "cb": cb, "cos": cos_t, "sin": sin_t,
        "conv_w_in": np.ascontiguousarray(inp["conv_w_in"], np.float32),
        "conv_w_out": np.ascontiguousarray(inp["conv_w_out"], np.float32),
        "attn_w_in": np.ascontiguousarray(inp["attn_w_in"], np.float32),
        "attn_w_out": np.ascontiguousarray(inp["attn_w_out"], np.float32),
        "mlp_w1": np.ascontiguousarray(inp["mlp_w1"], np.float32),
        "mlp_w2": np.ascontiguousarray(inp["mlp_w2"], np.float32),
    }
    in_maps = []
    for b in range(8):
        m = dict(shared)
        m["x"] = np.ascontiguousarray(x[b].T)
        in_maps.append(m)
    res = run_bass_kernel_spmd(nc, in_maps, core_ids=list(range(8)))
    out = np.stack([np.ascontiguousarray(res.results[b]["y"].T) for b in range(8)], axis=0)
    if debug:
        return out.astype(np.float32), [res.results[b]["ht_scr"] for b in range(8)]
    return out.astype(np.float32)


def kernel(**inputs):
    return run(inputs, ALL_STAGES)
```

```python
import numpy as np
from contextlib import ExitStack
import ml_dtypes
import concourse.bass as bass
import concourse.mybir as mybir
from concourse.bass_utils import run_bass_kernel_spmd

F32 = mybir.dt.float32
BF16 = mybir.dt.bfloat16
AF = mybir.ActivationFunctionType
ALU = mybir.AluOpType

D = 1024
S = 4096
NT = 512
NTL = S // NT
EPS = 1e-6
GROUPS = ((128, 1), (512, 4), (2048, 16))
ARENA_BYTES = 207 * 1024

C_MN = 0
C_LN = 32
C_CBI = 64
C_CWD = 96
C_CBD = 592
C_CLG = 608
C_CLB = 624
C_CBO = 640
C_AQN = 656
C_AKN = 658
NPAR = 660
K_ONES, K_BD, K_ROT, K_ID, K_MNEG, K_MNEG1 = 0, 128, 256, 384, 512, 1024
NCB = 1536


class Buf:
    def __init__(self, name, ap, arena=None, off=0):
        self.name = name
        self.ap = ap
        self.keys = [name]
        self.arena = arena
        self.off = off
        self.pages = ()

    def alias_bf16(self, nbytes):
        b = Buf(self.name, self.arena.t[:, self.off // 2:(self.off + nbytes) // 2], self.arena, self.off)
        b.pages = self.pages
        return b

    def v(self, a):
        return self.ap.rearrange("p (a b) -> p a b", a=a)


class Arena:
    def __init__(self, t, nbytes):
        self.t = t
        self.top = 0
        self.n = nbytes

    def alloc(self, name, nbytes, dtype=BF16):
        off = self.top
        sz = (nbytes + 255) // 256 * 256
        self.top += sz
        assert self.top <= self.n, (name, self.top, self.n)
        ap = self.t[:, off // 2:(off + nbytes) // 2]
        if dtype != BF16:
            ap = ap.bitcast(dtype)
        b = Buf(name, ap, self, off)
        b.pages = tuple(range(off // 256, (off + sz) // 256))
        return b


class Prog:
    def __init__(self, nc, es):
        self.nc = nc
        self.es = es
        self.engs = ("pe", "act", "dve", "pool", "sp")
        self.q = {e: [] for e in self.engs}
        self.csem = {}
        for e in ("pe", "act", "dve", "pool"):
            self.csem[e] = es.enter_context(nc.semaphore("c_" + e))
        self.ccnt = {e: 0 for e in self.csem}
        self.dsem = {}
        self.waited = {e: {} for e in self.engs}
        self.lastw = {}
        self.readers = {}
        self.psn = 0
        self.nops = 0
        self.keypages = {}
        self.prev_pages = {}
        self.cur_pages = {}
        self.epoch = 0
        self._pcache = {}

    def reg(self, key, buf):
        self.keypages[key] = buf.pages

    def new_epoch(self):
        for pg, d in self.cur_pages.items():
            pd = self.prev_pages.setdefault(pg, {})
            for kk, tok in d.items():
                if kk not in pd or pd[kk][1] < tok[1]:
                    pd[kk] = tok
        self.cur_pages = {}
        self.epoch += 1
        self._pcache = {}

    def _pages_of(self, items):
        out = []
        for x in items:
            if isinstance(x, Buf):
                if x.pages:
                    out.append((x.name, x.pages))
            elif x in self.keypages:
                out.append((x, self.keypages[x]))
        return out

    def _page_deps(self, plist):
        toks = {}
        for key, pages in plist:
            c = self._pcache.get(key)
            if c is None:
                c = {}
                for pg in pages:
                    for kk, tok in self.prev_pages.get(pg, {}).items():
                        if kk not in c or c[kk][1] < tok[1]:
                            c[kk] = tok
                self._pcache[key] = c
            for kk, tok in c.items():
                if kk not in toks or toks[kk][1] < tok[1]:
                    toks[kk] = tok
        return toks

    def _page_commit(self, plist, tok):
        kk = id(tok[0])
        for key, pages in plist:
            for pg in pages:
                d = self.cur_pages.setdefault(pg, {})
                if kk not in d or d[kk][1] < tok[1]:
                    d[kk] = tok

    @staticmethod
    def _keys(lst):
        out = []
        for x in lst:
            if hasattr(x, "keys") and not isinstance(x, dict):
                out.extend(x.keys)
            else:
                out.append(x)
        return out

    def _deps(self, e, reads, writes, plist=()):
        deps = {}

        def add(tok, war=False):
            sem, val, src = tok
            if src == "pe" and e == "pe":
                return
            if war and src == e and e in ("act", "dve", "pool"):
                return
            k = id(sem)
            if k not in deps or deps[k][1] < val:
                deps[k] = (sem, val)

        for r in reads:
            t = self.lastw.get(r)
            if t:
                add(t)
        for w in writes:
            t = self.lastw.get(w)
            if t:
                add(t)
            for t in self.readers.get(w, {}).values():
                add(t, war=True)
        if plist and self.prev_pages:
            for t in self._page_deps(plist).values():
                add(t)
        waits = []
        for k, (sem, val) in deps.items():
            if self.waited[e].get(k, 0) >= val:
                continue
            self.waited[e][k] = val
            waits.append((sem, val))
        return waits

    def _commit(self, tok, reads, writes):
        k = id(tok[0])
        for r in reads:
            d = self.readers.setdefault(r, {})
            if k not in d or d[k][1] < tok[1]:
                d[k] = tok
        for w in writes:
            self.lastw[w] = tok
            self.readers[w] = {}

    def op(self, e, fn, reads=(), writes=()):
        plist = self._pages_of(list(reads) + list(writes))
        reads = self._keys(reads)
        writes = self._keys(writes)
        waits = self._deps(e, reads, writes, plist)
        self.ccnt[e] += 1
        tok = (self.csem[e], self.ccnt[e], e)
        self.q[e].append((waits, fn, self.csem[e], False))
        self._commit(tok, reads, writes)
        self._page_commit(plist, tok)
        self.nops += 1

    def dma(self, e, out, in_, key, reads=(), writes=()):
        plist = self._pages_of(list(reads) + list(writes))
        reads = self._keys(reads)
        writes = self._keys(writes)
        waits = self._deps(e, reads, writes, plist)
        if key not in self.dsem:
            self.dsem[key] = [self.es.enter_context(self.nc.semaphore("d%d" % len(self.dsem))), 0]
        ent = self.dsem[key]
        ent[1] += 16
        tok = (ent[0], ent[1], "dma")

        def fn(eng, out=out, in_=in_):
            return eng.dma_start(out=out, in_=in_)

        self.q[e].append((waits, fn, ent[0], True))
        self._commit(tok, reads, writes)
        self._page_commit(plist, tok)
        self.nops += 1

    def barrier(self):
        toks = [(self.csem[e], self.ccnt[e]) for e in self.csem if self.ccnt[e] > 0]
        toks += [(s, c) for (s, c) in self.dsem.values() if c > 0]
        for e in self.engs:
            waits = []
            for sem, val in toks:
                k = id(sem)
                if self.waited[e].get(k, 0) >= val:
                    continue
                self.waited[e][k] = val
                waits.append((sem, val))
            if waits:
                self.q[e].append((waits, None, None, False))

    def replay(self, e, eng):
        for waits, fn, sem, is_dma in self.q[e]:
            for s, v in waits:
                eng.wait_ge(s, v)
            if fn is None:
                continue
            inst = fn(eng)
            inst.then_inc(sem, 16 if is_dma else 1)


class K:
    def __init__(self, nc, P, arena, ps_banks):
        self.nc = nc
        self.P = P
        self.A = arena
        self.ps = ps_banks
        self.psn = 0

    def psum(self):
        b = self.ps[self.psn % getattr(self, "npool", 6)]
        self.psn += 1
        return b

    def mm(self, out, pairs, reads, writes):
        def fn(eng, out=out, pairs=pairs):
            n = len(pairs)
            last = None
            for i, (l, r) in enumerate(pairs):
                last = eng.matmul(out, l, r, start=(i == 0), stop=(i == n - 1))
            return last
        self.P.op("pe", fn, reads, writes)

    def mm1(self, out, l, r, start, stop, reads, writes):
        def fn(eng):
            return eng.matmul(out, l, r, start=start, stop=stop)
        self.P.op("pe", fn, reads, writes)

    def act(self, out, in_, func, reads, writes, bias=None, scale=None):
        def fn(eng):
            kw = {}
            if bias is not None:
                kw["bias"] = bias
            if scale is not None:
                kw["scale"] = scale
            return eng.activation(out=out, in_=in_, func=func, **kw)
        self.P.op("act", fn, reads, writes)

    def stt(self, out, in0, scalar, in1, op0, op1, reads, writes):
        def fn(eng):
            return eng.scalar_tensor_tensor(out=out, in0=in0, scalar=scalar, in1=in1, op0=op0, op1=op1)
        self.P.op("dve", fn, reads, writes)

    def tt(self, e, out, in0, in1, op, reads, writes):
        def fn(eng):
            return eng.tensor_tensor(out=out, in0=in0, in1=in1, op=op)
        self.P.op(e, fn, reads, writes)

    def ts(self, e, out, in0, s1, s2, op0, op1, reads, writes):
        def fn(eng):
            if op1 is None:
                return eng.tensor_scalar(out=out, in0=in0, scalar1=s1, scalar2=None, op0=op0)
            return eng.tensor_scalar(out=out, in0=in0, scalar1=s1, scalar2=s2, op0=op0, op1=op1)
        self.P.op(e, fn, reads, writes)

    def recip(self, out, in_, reads, writes):
        def fn(eng):
            return eng.reciprocal(out=out, in_=in_)
        self.P.op("dve", fn, reads, writes)

    def copy(self, e, out, in_, reads, writes):
        if e == "act":
            def fn(eng):
                return eng.copy(out=out, in_=in_)
        else:
            def fn(eng):
                return eng.tensor_copy(out=out, in_=in_)
        self.P.op(e, fn, reads, writes)

    def memset(self, e, ap, val, writes):
        def fn(eng):
            return eng.memset(ap, val)
        self.P.op(e, fn, (), writes)


def ytile(Y, t):
    return Y.rearrange("(c p) n -> p c n", p=128)[:, :, t * NT:(t + 1) * NT]


def norm_sweep(k, src, srcname, gcol, dst, HT):
    P = k.P
    ones = k.cb.ap[:, K_ONES:K_ONES + 128]

    def load(t):
        xt = k.xt[t % 2]
        P.dma("sp", out=xt.v(8), in_=ytile(src, t), key=("ld", xt.name),
              reads=[(srcname, t)], writes=[xt])

    def comp(t):
        s = t % 2
        xt = k.xt[s]
        ps = k.psum()
        for c in range(8):
            sq = k.sq[c % 2]
            k.act(sq.ap, xt.v(8)[:, c, :], AF.Square, [xt], [sq])
            k.mm1(ps.ap, ones, sq.ap, c == 0, c == 7, [sq, k.cb], [ps])
        st = k.std[s]
        k.act(st.ap, ps.ap, AF.Ln, [ps, k.epsb], [st], bias=k.epsb.ap[:, 0:1], scale=1.0 / D)
        k.act(st.ap, st.ap, AF.Exp, [st], [st], scale=-0.5)
        if dst == "HT":
            ht = k.ht[s]
            for c in range(8):
                k.stt(ht.v(8)[:, c, :], xt.v(8)[:, c, :], k.par.ap[:, gcol + c:gcol + c + 1], st.ap,
                      ALU.mult, ALU.mult, [xt, st, k.par], [ht])
            P.dma("sp", out=ytile(HT, t), in_=ht.v(8), key=("st", ht.name),
                  reads=[ht], writes=[("HT", t)])
        else:
            for c in range(8):
                k.stt(dst.v(8)[:, c, t * NT:(t + 1) * NT], xt.v(8)[:, c, :],
                      k.par.ap[:, gcol + c:gcol + c + 1], st.ap,
                      ALU.mult, ALU.mult, [xt, st, k.par], [("hTall", t)])

    load(0)
    for t in range(NTL):
        if t + 1 < NTL:
            load(t + 1)
        comp(t)


class FusedNorm:
    def __init__(self, k, gcol, HT, staging, sqs, bank):
        self.k, self.gcol, self.HT, self.staging, self.sqs, self.bank = k, gcol, HT, staging, sqs, bank
        self.pend = []
        self.n = 0

    def chunk(self, t, c, xt):
        k = self.k
        sq = self.sqs[self.n % len(self.sqs)]
        self.n += 1
        k.act(sq.ap, xt.v(8)[:, c, :], AF.Square, [xt], [sq])
        self.pend.append((c, sq))
        if len(self.pend) > min(2, len(self.sqs) - 1):
            self._flush1()

    def _flush1(self):
        k = self.k
        c, sq = self.pend.pop(0)
        ones = k.cb.ap[:, K_ONES:K_ONES + 128]
        k.mm1(self.bank.ap, ones, sq.ap, c == 0, c == 7, [sq, k.cb], [self.bank])

    def finish(self, t, xt):
        k = self.k
        while self.pend:
            self._flush1()
        st = k.std[t % 2]
        k.act(st.ap, self.bank.ap, AF.Ln, [self.bank, k.epsb], [st], bias=k.epsb.ap[:, 0:1], scale=1.0 / D)
        k.act(st.ap, st.ap, AF.Exp, [st], [st], scale=-0.5)
        hb = self.staging[t % len(self.staging)]
        for c in range(8):
            k.stt(hb.v(8)[:, c, :], xt.v(8)[:, c, :], k.par.ap[:, self.gcol + c:self.gcol + c + 1], st.ap,
                  ALU.mult, ALU.mult, [xt, st, k.par], [hb])
        k.P.dma("sp", out=ytile(self.HT, t), in_=hb.v(8), key=("st", "fn", t % len(self.staging)),
                reads=[hb], writes=[("HT", t)])


def mlp_layer(k, l, Y, HT, W1d, W2d, prenormed=False, next_gcol=None):
    P = k.P
    A = k.A
    mark = A.top
    k.npool = 8
    k.ht = [A.alloc("mlp.ht%d" % i, 8192) for i in range(2)]
    w1 = [A.alloc("mlp.w1_%d" % i, 16384) for i in range(2)]
    w2 = [A.alloc("mlp.w2_%d" % i, 16384) for i in range(2)]
    z = [A.alloc("mlp.z%d" % i, 8192) for i in range(2)]
    sqv = [A.alloc("mlp.sqv%d" % i, 2048, F32) for i in range(3)]
    fnorm = None
    if next_gcol is not None:
        k.npool = 7
        hn = [A.alloc("mlp.hn%d" % i, 8192) for i in range(2)]
        fsq = [A.alloc("mlp.fsq%d" % i, 1024) for i in range(4)]
        fnorm = FusedNorm(k, next_gcol, HT, hn, fsq, k.ps[7])

    def load_w(q):
        s = q % 2
        src1 = W1d[l].rearrange("(c p) f -> p c f", p=128)[:, :, q * 1024:(q + 1) * 1024]
        src2 = W2d[l, q * 1024:(q + 1) * 1024, :].rearrange("(j p) d -> p j d", p=128)
        for h in range(2):
            P.dma("pool", out=w1[s].v(8)[:, 4 * h:4 * h + 4, :], in_=src1[:, 4 * h:4 * h + 4, :],
                  key=("ld", w1[s].name, h), reads=[], writes=[(w1[s].name, h)])
        for h in range(2):
            P.dma("pool", out=w2[s].v(8)[:, 4 * h:4 * h + 4, :], in_=src2[:, 4 * h:4 * h + 4, :],
                  key=("ld", w2[s].name, h), reads=[], writes=[(w2[s].name, h)])

    def wk(b):
        return [(b.name, 0), (b.name, 1)]
    for b_ in w1 + w2:
        for h_ in range(2):
            P.reg((b_.name, h_), b_)

    load_w(0)
    if not prenormed:
        norm_sweep(k, Y, "Y", C_LN + 8 * l, "HT", HT)
    items = [(q, t) for q in range(4) for t in range(NTL)]
    cnt = [0]

    def LH(i):
        q, t = items[i]
        s = i % 2
        P.dma("sp", out=k.ht[s].v(8), in_=ytile(HT, t), key=("ld", k.ht[s].name),
              reads=[("HT", t)], writes=[k.ht[s]])

    def LX(i):
        q, t = items[i]
        s = i % 2
        P.dma("sp", out=k.xt[s].v(8), in_=ytile(Y, t), key=("ld", k.xt[s].name),
              reads=[("Y", t)], writes=[k.xt[s]])

    def P1(i):
        q, t = items[i]
        s = i % 2
        ws = q % 2
        ht, zz = k.ht[s], z[s]
        for j in range(8):
            ps = k.psum()
            k.mm(ps.ap, [(w1[ws].v(8)[:, kc, j * 128:(j + 1) * 128], ht.v(8)[:, kc, :]) for kc in range(8)],
                 wk(w1[ws]) + [ht], [ps])
            sv = sqv[cnt[0] % 3]
            cnt[0] += 1
            k.act(sv.ap, ps.ap, AF.Square, [ps], [sv])
            k.stt(zz.v(8)[:, j, :], ps.ap, 0.0, sv.ap, ALU.is_gt, ALU.mult, [ps, sv], [zz])

    def P2(i):
        q, t = items[i]
        s = i % 2
        ws = q % 2
        xt, zz = k.xt[s], z[s]
        for o in range(8):
            ps = k.psum()
            k.mm(ps.ap, [(w2[ws].v(8)[:, j, o * 128:(o + 1) * 128], zz.v(8)[:, j, :]) for j in range(8)],
                 wk(w2[ws]) + [zz], [ps])
            k.tt("dve", xt.v(8)[:, o, :], xt.v(8)[:, o, :], ps.ap, ALU.add, [xt, ps], [xt])
            if fnorm is not None and q == 3:
                fnorm.chunk(t, o, xt)
        P.dma("sp", out=ytile(Y, t), in_=xt.v(8), key=("st", xt.name),
              reads=[xt], writes=[("Y", t)])
        if fnorm is not None and q == 3:
            fnorm.finish(t, xt)

    LH(0)
    LX(0)
    LH(1)
    P1(0)
    for i in range(len(items)):
        q, t = items[i]
        if t == (2 if q == 0 else 0) and q + 1 < 4:
            load_w(q + 1)
        if i + 1 < len(items):
            LX(i + 1)
            P1(i + 1)
        if i + 2 < len(items):
            LH(i + 2)
        P2(i)
    A.top = mark


def conv_layer(k, l, j, X, xname, Y, HT, Wi, Wo, prenormed=False, next_gcol=None):
    P = k.P
    A = k.A
    mark = A.top
    k.npool = 6
    k.ht = [A.alloc("cv.ht%d" % i, 8192) for i in range(2)]
    w_in = A.alloc("cv.w_in", 32768)
    w_out = A.alloc("cv.w_out", 16384)
    diag = A.alloc("cv.diag", 31 * 8 * 256)
    u = [A.alloc("cv.u%d" % i, 8 * 544 * 2) for i in range(2)]
    cbf = [A.alloc("cv.cbf%d" % i, 1024) for i in range(2)]
    csq = [A.alloc("cv.csq%d" % i, 1024) for i in range(2)]
    sg = [A.alloc("cv.sg%d" % i, 2048, F32) for i in range(2)]
    mean = A.alloc("cv.mean", 2048, F32)
    var = A.alloc("cv.var", 2048, F32)
    nmr = A.alloc("cv.nmr", 2048, F32)
    y1 = [A.alloc("cv.y1_%d" % i, 2048, F32) for i in range(2)]
    zb = [k.std[0], k.std[1]]
    cc = k.xt[1]
    xt = k.xt[0]

    w_in_k = [(w_in.name, h) for h in range(4)]
    w_out_k = [(w_out.name, h) for h in range(2)]
    for kk_ in w_in_k:
        P.reg(kk_, w_in)
    for kk_ in w_out_k:
        P.reg(kk_, w_out)
    srci = Wi[j].rearrange("(c p) f -> p c f", p=128)
    srco = Wo[j].rearrange("(c p) f -> p c f", p=128)

    def ld_in(h):
        P.dma("pool", out=w_in.v(8)[:, 2 * h:2 * h + 2, :], in_=srci[:, 2 * h:2 * h + 2, :],
              key=("ld", w_in.name, h), reads=[], writes=[(w_in.name, h)])

    def ld_out(h):
        P.dma("pool", out=w_out.v(8)[:, 4 * h:4 * h + 4, :], in_=srco[:, 4 * h:4 * h + 4, :],
              key=("ld", w_out.name, h), reads=[], writes=[(w_out.name, h)])
    ld_in(0)
    ld_in(1)
    ld_out(0)
    ld_out(1)
    ld_in(2)
    ld_in(3)
    for n_ in range(248):
        P.reg(("diag", n_), diag)
    ident = k.cb.ap[:, K_ID:K_ID + 128]
    ones = k.cb.ap[:, K_ONES:K_ONES + 128]
    dv = diag.v(248)
    n = 0
    for kk in range(31):
        for c in range(8):
            col = C_CWD + 248 * j + kk * 8 + c
            if n % 2 == 0:
                k.ts("dve", dv[:, kk * 8 + c, :], ident, k.par.ap[:, col:col + 1], None, ALU.mult, None,
                     [k.cb, k.par], [("diag", n)])
            else:
                k.act(dv[:, kk * 8 + c, :], ident, AF.Copy, [k.cb, k.par], [("diag", n)],
                      scale=k.par.ap[:, col:col + 1])
            n += 1
    diag_k = [("diag", n - 1), ("diag", n - 2)]

    if not prenormed:
        norm_sweep(k, X, xname, C_MN + 8 * l, "HT", HT)
    fnorm = None
    k.npool = 5
    fsq = [A.alloc("cv.fsq%d" % i, 1024) for i in range(2)]
    if next_gcol is not None:
        fnorm = FusedNorm(k, next_gcol, HT, [u[0].alias_bf16(8192), u[1].alias_bf16(8192)], fsq, k.ps[5])

    uv = [b.v(8) for b in u]
    bcol = C_CBI + 16 * j

    def L(t):
        ht = k.ht[t % 2]
        P.dma("sp", out=ht.v(8), in_=ytile(HT, t), key=("ld", ht.name),
              reads=[("HT", t)], writes=[ht])

    def LX(t):
        P.dma("sp", out=xt.v(8), in_=ytile(X, t), key=("ld", xt.name),
              reads=[(xname, t)], writes=[xt])

    def SA0(t):
        s = t % 2
        if t == 0:
            k.memset("pool", uv[s][:, :, 0:30], 0.0, [u[s]])
        else:
            k.copy("pool", uv[s][:, :, 0:30], uv[1 - s][:, :, 512:542], [u[1 - s]], [u[s]])

    def SAc(t, c):
        s = t % 2
        ht = k.ht[s]
        if True:
            psv = k.psum()
            k.mm(psv.ap, [(w_in.v(8)[:, kc, c * 128:(c + 1) * 128], ht.v(8)[:, kc, :]) for kc in range(8)],
                 w_in_k + [ht], [psv])
            psg = k.psum()
            k.mm(psg.ap, [(w_in.v(8)[:, kc, 1024 + c * 128:1024 + (c + 1) * 128], ht.v(8)[:, kc, :]) for kc in range(8)],
                 w_in_k + [ht], [psg])
            g = sg[c % 2]
            k.act(g.ap, psg.ap, AF.Sigmoid, [psg, k.par], [g], bias=k.par.ap[:, bcol + 8 + c:bcol + 9 + c])
            k.stt(uv[s][:, c, 30:542], psv.ap, k.par.ap[:, bcol + c:bcol + c + 1], g.ap, ALU.add, ALU.mult,
                  [psv, g, k.par], [u[s]])

    stats_pend = []

    def stats_flush(n_keep):
        while len(stats_pend) > n_keep:
            c, b1, b2 = stats_pend.pop(0)
            k.mm1(k.ps[6].ap, ones, b1.ap, c == 0, c == 7, [b1, k.cb], [k.ps[6]])
            k.mm1(k.ps[7].ap, ones, b2.ap, c == 0, c == 7, [b2, k.cb], [k.ps[7]])

    def SBc(t, c):
        s = t % 2
        ps = k.psum()
        k.mm(ps.ap, [(dv[:, kk * 8 + c, :], uv[s][:, c, kk:kk + 512]) for kk in range(31)],
             diag_k + [u[s]], [ps])
        stats_flush(1)
        bc = C_CBD + 8 * j + c
        k.ts("dve", cc.v(8)[:, c, :], ps.ap, k.par.ap[:, bc:bc + 1], None, ALU.add, None,
             [ps, k.par], [cc])
        b1, b2 = cbf[c % 2], csq[c % 2]
        k.act(b1.ap, cc.v(8)[:, c, :], AF.Copy, [cc], [b1])
        k.act(b2.ap, cc.v(8)[:, c, :], AF.Square, [cc], [b2])
        stats_pend.append((c, b1, b2))

    def SC0(t):
        ps_sum = k.ps[6]
        ps_sq = k.ps[7]
        k.ts("dve", mean.ap, ps_sum.ap, 1.0 / D, None, ALU.mult, None, [ps_sum], [mean])
        k.tt("dve", var.ap, mean.ap, mean.ap, ALU.mult, [mean], [var])
        k.stt(var.ap, ps_sq.ap, 1.0 / D, var.ap, ALU.mult, ALU.subtract, [ps_sq, var], [var])
        k.act(var.ap, var.ap, AF.Ln, [var, k.epsb], [var], bias=k.epsb.ap[:, 0:1])
        k.act(var.ap, var.ap, AF.Exp, [var], [var], scale=-0.5)
        k.tt("dve", nmr.ap, mean.ap, var.ap, ALU.mult, [mean, var], [nmr])

    def SCc(t, c):
        sb = k.ht[t % 2]
        if True:
            yy = y1[c % 2]
            k.tt("dve", yy.ap, cc.v(8)[:, c, :], var.ap, ALU.mult, [cc, var], [yy])
            k.tt("dve", yy.ap, yy.ap, nmr.ap, ALU.subtract, [yy, nmr], [yy])
            gc = C_CLG + 8 * j + c
            bc = C_CLB + 8 * j + c
            zz = zb[c % 2]
            k.act(zz.ap, yy.ap, AF.Sigmoid, [yy, k.par], [zz],
                  bias=k.par.ap[:, bc:bc + 1], scale=k.par.ap[:, gc:gc + 1])
            k.ts("dve", yy.ap, yy.ap, k.par.ap[:, gc:gc + 1], k.par.ap[:, bc:bc + 1], ALU.mult, ALU.add,
                 [yy, k.par], [yy])
            k.tt("dve", sb.v(8)[:, c, :], yy.ap, zz.ap, ALU.mult, [yy, zz], [sb])

    def SD(t):
        sb = k.ht[t % 2]
        for o in range(8):
            ps = k.psum()
            k.mm(ps.ap, [(w_out.v(8)[:, c, o * 128:(o + 1) * 128], sb.v(8)[:, c, :]) for c in range(8)],
                 w_out_k + [sb], [ps])
            bc = C_CBO + 8 * j + o
            k.stt(xt.v(8)[:, o, :], ps.ap, k.par.ap[:, bc:bc + 1], xt.v(8)[:, o, :], ALU.add, ALU.add,
                  [ps, xt, k.par], [xt])
            if fnorm is not None:
                fnorm.chunk(t, o, xt)
        P.dma("sp", out=ytile(Y, t), in_=xt.v(8), key=("st", xt.name),
              reads=[xt], writes=[("Y", t)])
        if fnorm is not None:
            fnorm.finish(t, xt)

    NPRE = 3
    L(0)
    SA0(0)
    for c in range(8):
        SAc(0, c)
    for c in range(NPRE):
        SBc(0, c)
    for t in range(NTL):
        if t + 1 < NTL:
            L(t + 1)
        LX(t)
        for c in range(NPRE, 8):
            SBc(t, c)
        stats_flush(0)
        SC0(t)
        if t + 1 < NTL:
            SA0(t + 1)
        for c in range(8):
            if t + 1 < NTL:
                SAc(t + 1, c)
            SCc(t, c)
        if t + 1 < NTL:
            for c in range(NPRE):
                SBc(t + 1, c)
        SD(t)
    A.top = mark


def attn_layer(k, l, j, Y, HT, OTs, Win, Wout, COSd, SINd, prenormed=False, next_gcol=None):
    P = k.P
    A = k.A
    mark = A.top
    k.npool = 8
    hT = A.alloc("at.hTall", 65536)
    wq = [A.alloc("at.wq%d" % i, 2048) for i in range(2)]
    wk = [A.alloc("at.wk%d" % i, 2048) for i in range(2)]
    wv = [A.alloc("at.wv%d" % i, 2048) for i in range(2)]
    QT = A.alloc("at.QT", 8192)
    KT = A.alloc("at.KT", 8192)
    VA = A.alloc("at.VA", 16384)
    cosb = [A.alloc("at.cos%d" % i, 2048, F32) for i in range(2)]
    sinb = [A.alloc("at.sin%d" % i, 2048, F32) for i in range(2)]
    sq3 = [A.alloc("at.sq%d" % i, 1024) for i in range(3)]
    rs = [A.alloc("at.rs%d" % i, 2048, F32) for i in range(2)]
    qn = [A.alloc("at.qn%d" % i, 1024) for i in range(3)]
    t1 = [A.alloc("at.t1_%d" % i, 2048, F32) for i in range(2)]
    t2 = [A.alloc("at.t2_%d" % i, 2048, F32) for i in range(2)]
    PTb = [A.alloc("at.PT%d" % i, 1024) for i in range(4)]
    DEN = [A.alloc("at.den%d" % i, 2048, F32) for i in range(2)]
    otst = [A.alloc("at.otst%d" % i, 1024) for i in range(2)]
    otl = [A.alloc("at.otl%d" % i, 4096) for i in range(2)]
    w_out = A.alloc("at.w_out", 8192)
    ACC = [k.xt[0], k.xt[1]]

    bd = k.cb.ap[:, K_BD:K_BD + 128]
    rot = k.cb.ap[:, K_ROT:K_ROT + 128]
    ident = k.cb.ap[:, K_ID:K_ID + 128]
    mneg = k.cb.ap[:, K_MNEG:K_MNEG + 512]
    mneg1 = k.cb.ap[:, K_MNEG1:K_MNEG1 + 512]
    hv = hT.v(8)
    allh = [("hTall", t) for t in range(NTL)]
    for kk_ in allh:
        P.reg(kk_, hT)
    for b0_ in range(0, 32, 4):
        P.reg(("VA", b0_), VA)

    src = Wout[j].rearrange("(c p) d -> p c d", p=128)
    P.dma("pool", out=w_out.v(4), in_=src, key=("ld", w_out.name), reads=[], writes=[w_out])
    wsrc = Win[j].rearrange("(c p) f -> p c f", p=128)

    def load_w(i, hp, g):
        s = i % 2
        for which, wb in enumerate((wq[s], wk[s], wv[s])):
            off = which * 1536 + g * 512 + hp * 128
            P.dma("pool", out=wb.v(8), in_=wsrc[:, :, off:off + 128], key=("ld", wb.name),
                  reads=[], writes=[wb])

    sweeps = [(hp, g) for hp in range(4) for g in (2, 1, 0)]
    pending_nz = []
    load_w(0, *sweeps[0])
    if prenormed:
        for t in range(NTL):
            P.dma("sp", out=hT.v(8)[:, :, t * NT:(t + 1) * NT], in_=ytile(HT, t), key=("ld", "hTall", t),
                  reads=[("HT", t)], writes=[("hTall", t)])
    else:
        norm_sweep(k, Y, "Y", C_MN + 8 * l, hT, None)
    VAv = VA.v(32)
    k.memset("pool", VAv[:, :, 64:192], 1.0, [VA])
    OTd = OTs.rearrange("(c p) n -> p c n", p=128)

    for i, (hp, g) in enumerate(sweeps):
        if i + 1 < len(sweeps):
            load_w(i + 1, *sweeps[i + 1])
        ws = i % 2
        d = GROUPS[g][1]
        nsub = S // d
        nb = nsub // 128

        def VG(b0):
            ps = k.psum()
            for bi in range(4):
                b = b0 + bi
                r, qb = b // nb, b % nb
                st_ = r + d * 128 * qb
                k.mm(ps.ap[:, bi * 128:(bi + 1) * 128],
                     [(hv[:, kc, st_:st_ + 127 * d + 1:d], wv[ws].v(8)[:, kc, :]) for kc in range(8)],
                     [wv[ws]] + allh, [ps])
            psv = ps.ap.rearrange("p (b c) -> p b c", b=4)
            k.copy("act", VAv[:, b0:b0 + 4, 0:64], psv[:, :, 0:64], [ps], [("VA", b0)])
            k.copy("dve", VAv[:, b0:b0 + 4, 192:256], psv[:, :, 64:128], [ps], [("VA", b0)])

        pitems = [(t, w) for t in range(NTL) for w in range(2)]
        pst = {}

        def PA(n):
            t, w = pitems[n]
            s = t % 2
            if w == 0:
                P.dma("sp", out=cosb[s].ap, in_=COSd[:, t * NT:(t + 1) * NT], key=("ld", cosb[s].name),
                      reads=[], writes=[cosb[s]])
                P.dma("sp", out=sinb[s].ap, in_=SINd[:, t * NT:(t + 1) * NT], key=("ld", sinb[s].name),
                      reads=[], writes=[sinb[s]])
            wb = (wq[ws], wk[ws])[w]
            ps = k.psum()
            k.mm(ps.ap, [(wb.v(8)[:, kc, :], hv[:, kc, t * NT:(t + 1) * NT]) for kc in range(8)],
                 [wb, ("hTall", t)], [ps])
            sq = sq3[n % 3]
            k.act(sq.ap, ps.ap, AF.Square, [ps], [sq])
            pst[n] = ps

        def PB(n):
            t, w = pitems[n]
            gcol = (C_AQN + j, C_AKN + j)[w]
            ps = pst[n]
            sq = sq3[n % 3]
            ps2 = k.psum()
            k.mm1(ps2.ap, bd, sq.ap, True, True, [sq, k.cb], [ps2])
            r_ = rs[n % 2]
            k.act(r_.ap, ps2.ap, AF.Ln, [ps2, k.epsb], [r_], bias=k.epsb.ap[:, 0:1], scale=1.0 / 64)
            k.act(r_.ap, r_.ap, AF.Exp, [r_], [r_], scale=-0.5)
            q_ = qn[n % 3]
            k.stt(q_.ap, ps.ap, k.par.ap[:, gcol:gcol + 1], r_.ap, ALU.mult, ALU.mult,
                  [ps, r_, k.par], [q_])

        def PC(n):
            t, w = pitems[n]
            s = t % 2
            dstb = (QT, KT)[w]
            q_ = qn[n % 3]
            ps3 = k.psum()
            k.mm1(ps3.ap, rot, q_.ap, True, True, [q_, k.cb], [ps3])
            a_, b_ = t1[n % 2], t2[n % 2]
            k.tt("pool", a_.ap, q_.ap, cosb[s].ap, ALU.mult, [q_, cosb[s]], [a_])
            k.tt("dve", b_.ap, ps3.ap, sinb[s].ap, ALU.mult, [ps3, sinb[s]], [b_])
            l0 = t * NT // d
            ln = NT // d
            if d == 1:
                dst = dstb.ap[:, t * NT:(t + 1) * NT]
                av, bv = a_.ap, b_.ap
            else:
                dst = dstb.ap.rearrange("p (r l) -> p r l", r=d)[:, :, l0:l0 + ln]
                av = a_.ap.rearrange("p (l r) -> p r l", r=d)
                bv = b_.ap.rearrange("p (l r) -> p r l", r=d)
            k.tt("pool" if w == 0 else "dve", dst, av, bv, ALU.add, [a_, b_], [dstb])

        npi = len(pitems)
        for n in range(npi + 2):
            if n < npi:
                PA(n)
            if n % 2 == 1 and pending_nz:
                pending_nz.pop(0)()
            if n % 2 == 0 and n // 2 < 8:
                VG(4 * (n // 2))
            if 0 <= n - 1 < npi:
                PB(n - 1)
            if 0 <= n - 2 < npi:
                PC(n - 2)
        VAk = [VA] + [("VA", b0) for b0 in range(0, 32, 4)]

        aitems = [(h, r, qb) for r in range(d) for qb in range(0, nb, 2) for h in range(2)]
        ast = {}

        def AS(n):
            h, r, qb = aitems[n]
            rows = slice(0, 64) if h == 0 else slice(64, 128)
            b0 = r * nb + qb
            b1 = b0 + 1
            pss = k.psum()
            mlist = [(pss.ap, ident, (mneg if qb > 0 else mneg1))]
            if qb > 0:
                mlist.append((pss.ap[:, 0:128], KT.ap[rows, (b0 - 1) * 128:b0 * 128], QT.ap[rows, b0 * 128:(b0 + 1) * 128]))
            mlist.append((pss.ap[:, 128:256], KT.ap[rows, b0 * 128:(b0 + 1) * 128], QT.ap[rows, b0 * 128:(b0 + 1) * 128]))
            mlist.append((pss.ap[:, 256:384], KT.ap[rows, b0 * 128:(b0 + 1) * 128], QT.ap[rows, b1 * 128:(b1 + 1) * 128]))
            mlist.append((pss.ap[:, 384:512], KT.ap[rows, b1 * 128:(b1 + 1) * 128], QT.ap[rows, b1 * 128:(b1 + 1) * 128]))

            def fn(eng, mlist=mlist):
                last = None
                nn = len(mlist)
                for ii, (o_, l_, r_) in enumerate(mlist):
                    last = eng.matmul(o_, l_, r_, start=(ii == 0), stop=(ii == nn - 1))
                return last
            P.op("pe", fn, [KT, QT, k.cb], [pss])
            PT = PTb[n % 4]
            k.act(PT.ap, pss.ap, AF.Exp, [pss], [PT], scale=0.125)

        def AV(n):
            h, r, qb = aitems[n]
            vsel = slice(0, 128) if h == 0 else slice(128, 256)
            acc = ACC[h]
            b0 = r * nb + qb
            b1 = b0 + 1
            PT = PTb[n % 4]
            pso = k.psum()
            if qb > 0:
                k.mm(pso.ap[:, 0:128], [(VAv[:, b0 - 1, vsel], PT.ap[:, 0:128]),
                                        (VAv[:, b0, vsel], PT.ap[:, 128:256])], VAk + [PT], [pso])
            else:
                k.mm(pso.ap[:, 0:128], [(VAv[:, b0, vsel], PT.ap[:, 128:256])], VAk + [PT], [pso])
            k.mm(pso.ap[:, 128:256], [(VAv[:, b0, vsel], PT.ap[:, 256:384]),
                                      (VAv[:, b1, vsel], PT.ap[:, 384:512])], VAk + [PT], [pso])
            st_ = r + d * 128 * qb
            accv = acc.ap[:, st_:st_ + 255 * d + 1:d]
            if g == 2:
                k.copy("act", accv, pso.ap[:, 0:256], [pso], [acc])
            else:
                k.tt("dve", accv, accv, pso.ap[:, 0:256], ALU.add, [acc, pso], [acc])

        nai = len(aitems)
        SK = 2
        for n in range(nai + SK):
            if n < nai:
                AS(n)
            if 0 <= n - SK < nai:
                AV(n - SK)
        if g != 0:
            continue
        def make_nz(t, hp=hp):
            def nz():
                s = t % 2
                cs = slice(t * NT, (t + 1) * NT)
                dn = DEN[s]
                P.dma("sp", out=dn.ap[0:64, :], in_=ACC[0].ap[64:128, cs], key=("ld", dn.name),
                      reads=[ACC[0]], writes=[dn])
                P.dma("sp", out=dn.ap[64:128, :], in_=ACC[1].ap[0:64, cs], key=("ld", dn.name),
                      reads=[ACC[1]], writes=[dn])
                k.act(dn.ap, dn.ap, AF.Ln, [dn], [dn])
                k.act(dn.ap, dn.ap, AF.Exp, [dn], [dn], scale=-1.0)
                ot = otst[s]
                k.tt("dve", ot.ap[0:64, :], ACC[0].ap[0:64, cs], dn.ap[0:64, :], ALU.mult, [ACC[0], dn], [ot])
                k.tt("pool", ot.ap[64:128, :], ACC[1].ap[64:128, cs], dn.ap[64:128, :], ALU.mult,
                     [ACC[1], dn], [ot])
                P.dma("sp", out=OTd[:, hp, cs], in_=ot.ap, key=("st", ot.name), reads=[ot], writes=[("OT", t)])
            return nz
        pending_nz.extend(make_nz(t) for t in range(NTL))
    while pending_nz:
        pending_nz.pop(0)()

    def LO(t):
        s = t % 2
        P.dma("sp", out=otl[s].v(4), in_=OTd[:, 0:4, t * NT:(t + 1) * NT], key=("ld", otl[s].name),
              reads=[("OT", t)], writes=[otl[s]])
        P.dma("sp", out=k.xt[s].v(8), in_=ytile(Y, t), key=("ld", k.xt[s].name),
              reads=[("Y", t)], writes=[k.xt[s]])

    fnorm = None
    if next_gcol is not None:
        k.npool = 7
        fnorm = FusedNorm(k, next_gcol, HT, [QT, KT], sq3 + qn, k.ps[7])
    LO(0)
    for t in range(NTL):
        s = t % 2
        if t + 1 < NTL:
            LO(t + 1)
        xt = k.xt[s]
        for o in range(8):
            ps = k.psum()
            k.mm(ps.ap, [(w_out.v(4)[:, c, o * 128:(o + 1) * 128], otl[s].v(4)[:, c, :]) for c in range(4)],
                 [w_out, otl[s]], [ps])
            k.tt("dve", xt.v(8)[:, o, :], xt.v(8)[:, o, :], ps.ap, ALU.add, [xt, ps], [xt])
            if fnorm is not None:
                fnorm.chunk(t, o, xt)
        P.dma("sp", out=ytile(Y, t), in_=xt.v(8), key=("st", xt.name), reads=[xt], writes=[("Y", t)])
        if fnorm is not None:
            fnorm.finish(t, xt)
    A.top = mark


def build(stages, debug=False):
    nc = bass.Bass("TRN2", target_bir_lowering=False)

    def dten(name, shape, dtype, kind):
        return nc.dram_tensor(name, shape, dtype, kind=kind).ap()

    X = dten("x", [D, S], F32, "ExternalInput")
    Y = dten("y", [D, S], F32, "ExternalOutput")
    HT = dten("ht_scr", [D, S], BF16, "ExternalOutput" if debug else "Internal")
    OTs = dten("ot_scr", [512, S], BF16, "Internal")
    PARd = dten("par", [128, NPAR], F32, "ExternalInput")
    CBd = dten("cb", [128, NCB], BF16, "ExternalInput")
    COSd = dten("cos", [128, S], F32, "ExternalInput")
    SINd = dten("sin", [128, S], F32, "ExternalInput")
    Wi = dten("conv_w_in", [2, D, 2 * D], F32, "ExternalInput")
    Wo = dten("conv_w_out", [2, D, D], F32, "ExternalInput")
    Win = dten("attn_w_in", [2, D, 4608], F32, "ExternalInput")
    Wout = dten("attn_w_out", [2, 512, D], F32, "ExternalInput")
    W1d = dten("mlp_w1", [4, D, 4 * D], F32, "ExternalInput")
    W2d = dten("mlp_w2", [4, 4 * D, D], F32, "ExternalInput")

    with ExitStack() as es:
        arena_t = es.enter_context(nc.sbuf_tensor("arena", [128, ARENA_BYTES // 2], BF16))
        banks = []
        for i in range(8):
            pt = es.enter_context(nc.psum_tensor("ps%d" % i, [128, 512], F32))
            banks.append(Buf(("ps", i), pt[:]))
        P = Prog(nc, es)
        A = Arena(arena_t, ARENA_BYTES)
        k = K(nc, P, A, banks)
        k.dbg = debug if isinstance(debug, str) else None
        k.par = A.alloc("par", NPAR * 4, F32)
        k.cb = A.alloc("cb", NCB * 2)
        k.epsb = A.alloc("epsb", 4, F32)
        k.xt = [A.alloc("xt%d" % i, 16384, F32) for i in range(2)]
        k.std = [A.alloc("std%d" % i, 2048, F32) for i in range(2)]
        k.sq = [A.alloc("sq%d" % i, 1024) for i in range(2)]
        P.dma("sp", out=k.par.ap, in_=PARd, key=("ld", "par"), reads=[], writes=[k.par])
        P.dma("sp", out=k.cb.ap, in_=CBd, key=("ld", "cb"), reads=[], writes=[k.cb])
        k.memset("dve", k.epsb.ap, EPS, [k.epsb])

        first = True
        for si, st in enumerate(stages):
            nxt = stages[si + 1] if si + 1 < len(stages) else None
            ng = None
            if nxt is not None:
                ng = (C_LN if nxt[0] == "mlp" else C_MN) + 8 * nxt[1]
            pn = not first
            if st[0] == "conv":
                _, l, j = st
                conv_layer(k, l, j, X if first else Y, "X" if first else "Y", Y, HT, Wi, Wo,
                           prenormed=pn, next_gcol=ng)
            elif st[0] == "attn":
                _, l, j = st
                assert not first
                attn_layer(k, l, j, Y, HT, OTs, Win, Wout, COSd, SINd, prenormed=pn, next_gcol=ng)
            elif st[0] == "mlp":
                _, l = st
                assert not first
                mlp_layer(k, l, Y, HT, W1d, W2d, prenormed=pn, next_gcol=ng)
            first = False
            P.new_epoch()
        P.barrier()

        with nc.Block() as block:
            @block.sync
            def _(eng):
                P.replay("sp", eng)

            @block.tensor
            def _(eng):
                P.replay("pe", eng)

            @block.scalar
            def _(eng):
                P.replay("act", eng)

            @block.vector
            def _(eng):
                P.replay("dve", eng)

            @block.gpsimd
            def _(eng):
                P.replay("pool", eng)
    return nc


ALL_STAGES = []
for _l in range(4):
    ALL_STAGES.append(("conv", _l, _l // 2) if _l % 2 == 0 else ("attn", _l, _l // 2))
    ALL_STAGES.append(("mlp", _l))


def vec8(v):
    return np.ascontiguousarray(np.asarray(v, np.float32).reshape(8, 128).T)


def host_consts(inp):
    par = np.zeros((128, NPAR), np.float32)
    for l in range(4):
        par[:, C_MN + 8 * l:C_MN + 8 * l + 8] = vec8(inp["mixer_norm"][l])
        par[:, C_LN + 8 * l:C_LN + 8 * l + 8] = vec8(inp["mlp_norm"][l])
    for j in range(2):
        par[:, C_CBI + 16 * j:C_CBI + 16 * j + 16] = np.asarray(inp["conv_b_in"][j], np.float32).reshape(16, 128).T
        w = np.asarray(inp["conv_w_dw"][j], np.float32)
        par[:, C_CWD + 248 * j:C_CWD + 248 * (j + 1)] = w.reshape(31, 8, 128).transpose(2, 0, 1).reshape(128, 248)
        par[:, C_CBD + 8 * j:C_CBD + 8 * j + 8] = vec8(inp["conv_b_dw"][j])
        par[:, C_CLG + 8 * j:C_CLG + 8 * j + 8] = vec8(inp["conv_ln_g"][j])
        par[:, C_CLB + 8 * j:C_CLB + 8 * j + 8] = vec8(inp["conv_ln_b"][j])
        par[:, C_CBO + 8 * j:C_CBO + 8 * j + 8] = vec8(inp["conv_b_out"][j])
        par[:, C_AQN + j] = np.tile(np.asarray(inp["attn_q_norm"][j], np.float32), 2)
        par[:, C_AKN + j] = np.tile(np.asarray(inp["attn_k_norm"][j], np.float32), 2)
    cb = np.zeros((128, NCB), np.float32)
    cb[:, K_ONES:K_ONES + 128] = 1.0
    p = np.arange(128)
    cb[:, K_BD:K_BD + 128] = (p[:, None] // 64 == p[None, :] // 64)
    partner = np.where(p % 64 < 32, p + 32, p - 32)
    cb[:, K_ROT:K_ROT + 128] = (p[:, None] == partner[None, :])
    cb[:, K_ID:K_ID + 128] = np.eye(128)
    mprev = (p[:, None] >= p[None, :])
    mcur = (p[:, None] <= p[None, :])
    NEG = -30000.0
    mp = np.where(mprev, 0.0, NEG)
    mc = np.where(mcur, 0.0, NEG)
    cb[:, K_MNEG:K_MNEG + 512] = np.concatenate([mp, mc, mp, mc], axis=1)
    cb[:, K_MNEG1:K_MNEG1 + 512] = np.concatenate([np.full((128, 128), NEG), mc, mp, mc], axis=1)
    pos = np.arange(S, dtype=np.float32)
    inv = (np.float32(10000.0) ** (-np.arange(0, 64, 2, dtype=np.float32) / np.float32(64))).astype(np.float32)
    ang = (pos[None, :] * inv[:, None]).astype(np.float32)
    cosr = np.cos(ang).astype(np.float32)
    sinr = np.sin(ang).astype(np.float32)
    idx = p % 32
    sign = np.where(p % 64 < 32, -1.0, 1.0).astype(np.float32)
    cos_t = np.ascontiguousarray(cosr[idx])
    sin_t = np.ascontiguousarray(sinr[idx] * sign[:, None])
    return par, cb.astype(ml_dtypes.bfloat16), cos_t, sin_t


_NC_CACHE = {}


def run(inputs, stages, debug=False):
    inp = {k_: np.asarray(v) for k_, v in inputs.items()}
    key = tuple(stages)
    if key not in _NC_CACHE:
        _NC_CACHE[key] = build(stages, debug)
    nc = _NC_CACHE[key]
    par, cb, cos_t, sin_t = host_consts(inp)
    x = np.asarray(inp["x"], np.float32)
    shared = {
        "par": par, "cb": cb, "cos": cos_t, "sin": sin_t,
        "conv_w_in": np.ascontiguousarray(inp["conv_w_in"], np.float32),
        "conv_w_out": np.ascontiguousarray(inp["conv_w_out"], np.float32),
        "attn_w_in": np.ascontiguousarray(inp["attn_w_in"], np.float32),
        "attn_w_out": np.ascontiguousarray(inp["attn_w_out"], np.float32),
        "mlp_w1": np.ascontiguousarray(inp["mlp_w1"], np.float32),
        "mlp_w2": np.ascontiguousarray(inp["mlp_w2"], np.float32),
    }
    in_maps = []
    for b in range(8):
        m = dict(shared)
        m["x"] = np.ascontiguousarray(x[b].T)
        in_maps.append(m)
    res = run_bass_kernel_spmd(nc, in_maps, core_ids=list(range(8)))
    out = np.stack([np.ascontiguousarray(res.results[b]["y"].T) for b in range(8)], axis=0)
    if debug:
        return out.astype(np.float32), [res.results[b]["ht_scr"] for b in range(8)]
    return out.astype(np.float32)


def kernel(**inputs):
    return run(inputs, ALL_STAGES)
```
